# Optimizing a Trainium2 kernel written in Bass

```python
import jax, jax.numpy as jnp
from jax import lax
import numpy as np

D_MODEL = 1024
BATCH = 16
SEQ = 2048
DEPTH = 2

N_A_LAYERS = DEPTH // 2
N_B_LAYERS = DEPTH - N_A_LAYERS
D_FF = 2816
CHUNK = 128
GMLP_HALF = 2 * D_MODEL
GMLP_GROUPS = 16
GMLP_GROUP_DIM = GMLP_HALF // GMLP_GROUPS
N_HEADS = 8
QK_NOPE = 128
QK_ROPE = 64
V_DIM = 128
KV_RANK = 256
Q_RANK = 512
Q_BLOCK = 128
ROPE_THETA = 10000.0
RMS_EPS = 1e-6
LN_EPS = 1e-5
NEG_INF = -1e30

kernel_name = "yoco_gmlp_mla_macaron_sandwich"


def rms_norm(x, g):
    x32 = x.astype(jnp.float32)
    y = x32 * lax.rsqrt(jnp.mean(x32 * x32, axis=-1, keepdims=True) + RMS_EPS) * g.astype(jnp.float32)
    return y.astype(x.dtype)


def layer_norm(x, g, b):
    x32 = x.astype(jnp.float32)
    mu = jnp.mean(x32, axis=-1, keepdims=True)
    xc = x32 - mu
    y = xc * lax.rsqrt(jnp.mean(xc * xc, axis=-1, keepdims=True) + LN_EPS)
    return (y * g.astype(jnp.float32) + b.astype(jnp.float32)).astype(x.dtype)


def swiglu(n, w_gate, w_up, w_down):
    return (jax.nn.silu(n @ w_gate) * (n @ w_up)) @ w_down


def rope_tables(positions):
    inv_freq = ROPE_THETA ** (-jnp.arange(0, QK_ROPE, 2, dtype=jnp.float32) / QK_ROPE)
    ang = positions.astype(jnp.float32)[..., None] * inv_freq
    return jnp.cos(ang), jnp.sin(ang)


def apply_rope(x, cos, sin):
    cos = cos.astype(x.dtype)
    sin = sin.astype(x.dtype)
    x1, x2 = jnp.split(x, 2, axis=-1)
    return jnp.concatenate([x1 * cos - x2 * sin, x2 * cos + x1 * sin], axis=-1)


def gmlp_mixer(n, w_in, ln_g, ln_b, w_s, b_s, w_out):
    b, s, _ = n.shape
    z = jax.nn.gelu(n @ w_in)
    u, v = jnp.split(z, 2, axis=-1)
    v = layer_norm(v, ln_g, ln_b)
    v = v.reshape(b, s // CHUNK, CHUNK, GMLP_GROUPS, GMLP_GROUP_DIM)
    causal = jnp.tril(jnp.ones((CHUNK, CHUNK), dtype=w_s.dtype))
    w = w_s * causal
    sv = jnp.einsum('gtc,bncgd->bntgd', w, v) + jnp.transpose(b_s)[None, None, :, :, None]
    return (u * sv.reshape(b, s, GMLP_HALF)) @ w_out


def shared_kv(h, kv_norm_g, w_dkv, kv_a_norm_g, w_ukv, cos, sin):
    b, s, _ = h.shape
    n = rms_norm(h, kv_norm_g)
    ckv = n @ w_dkv
    c, k_rope = ckv[..., :KV_RANK], ckv[..., KV_RANK:]
    c = rms_norm(c, kv_a_norm_g)
    k_rope = apply_rope(k_rope, cos, sin)
    kv = (c @ w_ukv).reshape(b, s, N_HEADS, QK_NOPE + V_DIM)
    return kv[..., :QK_NOPE], k_rope, kv[..., QK_NOPE:]


def mla_mixer(n, k_nope, k_rope, v, w_dq, q_norm_g, w_uq, w_o, cos, sin):
    b, s, _ = n.shape
    q = (rms_norm(n @ w_dq, q_norm_g) @ w_uq).reshape(b, s, N_HEADS, QK_NOPE + QK_ROPE)
    q_nope = q[..., :QK_NOPE]
    q_rope = apply_rope(q[..., QK_NOPE:], cos[:, :, None, :], sin[:, :, None, :])
    scale = (QK_NOPE + QK_ROPE) ** -0.5
    outs = []
    for blk in range(s // Q_BLOCK):
        q0, q1 = blk * Q_BLOCK, (blk + 1) * Q_BLOCK
        sc = (jnp.einsum('bqhd,bkhd->bhqk', q_nope[:, q0:q1], k_nope[:, :q1])
              + jnp.einsum('bqhr,bkr->bhqk', q_rope[:, q0:q1], k_rope[:, :q1]))
        sc = sc.astype(jnp.float32) * scale
        q_idx = q0 + jnp.arange(Q_BLOCK)
        k_idx = jnp.arange(q1)
        sc = jnp.where(k_idx[None, :] <= q_idx[:, None], sc, NEG_INF)
        p = jax.nn.softmax(sc, axis=-1).astype(v.dtype)
        outs.append(jnp.einsum('bhqk,bkhd->bqhd', p, v[:, :q1]))
    o = jnp.concatenate(outs, axis=1).reshape(b, s, N_HEADS * V_DIM)
    return o @ w_o


def setup_inputs(seed: int = 0) -> dict:
    key = jax.random.key(seed)
    ks = iter(jax.random.split(key, 40))

    def w(shape, fan_in):
        return jax.random.normal(next(ks), shape, jnp.float32) * (fan_in ** -0.5)

    def g(shape):
        return 1.0 + 0.02 * jax.random.normal(next(ks), shape, jnp.float32)

    x = jax.random.normal(next(ks), (BATCH, SEQ, D_MODEL), jnp.float32)
    offs = jax.random.randint(next(ks), (BATCH, 1), 0, 1024, dtype=jnp.int32)
    positions = (jnp.arange(SEQ, dtype=jnp.int32)[None, :] + offs).astype(jnp.int32)
    return {
        "x": x,
        "positions": positions,
        "ffn_pre_g": g((DEPTH, 2, D_MODEL)),
        "ffn_post_g": g((DEPTH, 2, D_MODEL)),
        "ffn_w_gate": w((DEPTH, 2, D_MODEL, D_FF), D_MODEL),
        "ffn_w_up": w((DEPTH, 2, D_MODEL, D_FF), D_MODEL),
        "ffn_w_down": w((DEPTH, 2, D_FF, D_MODEL), D_FF),
        "mix_pre_g": g((DEPTH, D_MODEL)),
        "mix_post_g": g((DEPTH, D_MODEL)),
        "gmlp_w_in": w((N_A_LAYERS, D_MODEL, 2 * GMLP_HALF), D_MODEL),
        "gmlp_ln_g": g((N_A_LAYERS, GMLP_HALF)),
        "gmlp_ln_b": 0.02 * jax.random.normal(next(ks), (N_A_LAYERS, GMLP_HALF), jnp.float32),
        "gmlp_w_s": w((N_A_LAYERS, GMLP_GROUPS, CHUNK, CHUNK), CHUNK),
        "gmlp_b_s": g((N_A_LAYERS, GMLP_GROUPS, CHUNK)),
        "gmlp_w_out": w((N_A_LAYERS, GMLP_HALF, D_MODEL), GMLP_HALF),
        "kv_norm_g": g((D_MODEL,)),
        "w_dkv": w((D_MODEL, KV_RANK + QK_ROPE), D_MODEL),
        "kv_a_norm_g": g((KV_RANK,)),
        "w_ukv": w((KV_RANK, N_HEADS * (QK_NOPE + V_DIM)), KV_RANK),
        "mla_w_dq": w((N_B_LAYERS, D_MODEL, Q_RANK), D_MODEL),
        "mla_q_norm_g": g((N_B_LAYERS, Q_RANK)),
        "mla_w_uq": w((N_B_LAYERS, Q_RANK, N_HEADS * (QK_NOPE + QK_ROPE)), Q_RANK),
        "mla_w_o": w((N_B_LAYERS, N_HEADS * V_DIM, D_MODEL), N_HEADS * V_DIM),
    }


def reference(x, positions, ffn_pre_g, ffn_post_g, ffn_w_gate, ffn_w_up, ffn_w_down,
              mix_pre_g, mix_post_g, gmlp_w_in, gmlp_ln_g, gmlp_ln_b, gmlp_w_s, gmlp_b_s,
              gmlp_w_out, kv_norm_g, w_dkv, kv_a_norm_g, w_ukv, mla_w_dq, mla_q_norm_g,
              mla_w_uq, mla_w_o):
    cos, sin = rope_tables(positions)
    h = x
    k_nope = k_rope = v = None
    for layer in range(DEPTH):
        f = swiglu(rms_norm(h, ffn_pre_g[layer, 0]), ffn_w_gate[layer, 0], ffn_w_up[layer, 0], ffn_w_down[layer, 0])
        h = h + 0.5 * rms_norm(f, ffn_post_g[layer, 0])
        n = rms_norm(h, mix_pre_g[layer])
        if layer < N_A_LAYERS:
            m = gmlp_mixer(n, gmlp_w_in[layer], gmlp_ln_g[layer], gmlp_ln_b[layer],
                           gmlp_w_s[layer], gmlp_b_s[layer], gmlp_w_out[layer])
        else:
            j = layer - N_A_LAYERS
            m = mla_mixer(n, k_nope, k_rope, v, mla_w_dq[j], mla_q_norm_g[j], mla_w_uq[j], mla_w_o[j], cos, sin)
        h = h + rms_norm(m, mix_post_g[layer])
        f = swiglu(rms_norm(h, ffn_pre_g[layer, 1]), ffn_w_gate[layer, 1], ffn_w_up[layer, 1], ffn_w_down[layer, 1])
        h = h + 0.5 * rms_norm(f, ffn_post_g[layer, 1])
        if layer == N_A_LAYERS - 1:
            k_nope, k_rope, v = shared_kv(h, kv_norm_g, w_dkv, kv_a_norm_g, w_ukv, cos, sin)
    return h
```

```python
import numpy as np
import concourse.bass as bass
import concourse.mybir as mybir
from concourse.bass_utils import run_bass_kernel_spmd

F32 = mybir.dt.float32
BF16 = mybir.dt.bfloat16
I32 = mybir.dt.int32
ALU = mybir.AluOpType
AF = mybir.ActivationFunctionType
AX = mybir.AxisListType


class Buf:
    __slots__ = ("name", "w", "r", "excl")

    def __init__(self, name, excl=False):
        self.name = name
        self.w = None
        self.r = {}
        self.excl = excl


class Tile_:
    __slots__ = ("t", "b")

    def __init__(self, t, name):
        self.t = t
        self.b = Buf(name)


class Chan:
    __slots__ = ("sem", "n", "idx")

    def __init__(self, sem, idx):
        self.sem = sem
        self.n = 0
        self.idx = idx


class Ev:
    __slots__ = ("kind", "key", "val", "op")

    def __init__(self, kind, key, val, op):
        self.kind = kind
        self.key = key
        self.val = val
        self.op = op


class Op:
    __slots__ = ("eng", "fn", "waits", "signal", "chan", "sval")

    def __init__(self, eng, fn, chan):
        self.eng = eng
        self.fn = fn
        self.waits = []
        self.signal = False
        self.chan = chan
        self.sval = 0


ENGS = ("pe", "act", "dve", "pool", "sp")


class Sched:
    def __init__(self, nc):
        self.nc = nc
        self.ops = {e: [] for e in ENGS}
        self.seen = {e: {} for e in ENGS}
        self.chans = []
        self._stack = []
        self.esem = {}
        for e in ENGS:
            cm = nc.semaphore("s_" + e)
            self.esem[e] = cm.__enter__()
            self._stack.append(cm)
        self.nps = 0

    def sb(self, name, shape, dt):
        cm = self.nc.sbuf_tensor(name, list(shape), dt)
        t = cm.__enter__()
        self._stack.append(cm)
        return Tile_(t, name)

    def ps(self, name, shape=(128, 512), dt=F32):
        cm = self.nc.psum_tensor(name, list(shape), dt)
        t = cm.__enter__()
        self._stack.append(cm)
        tl = Tile_(t, name)
        tl.b.excl = True
        return tl

    def chan(self):
        cm = self.nc.semaphore("c%d" % len(self.chans))
        s = cm.__enter__()
        self._stack.append(cm)
        c = Chan(s, len(self.chans))
        self.chans.append(c)
        return c

    def _record(self, eng, fn, reads, writes, chan):
        o = Op(eng, fn, chan)
        deps = []
        for b in reads:
            if b.w is not None:
                deps.append(b.w)
            if b.excl:
                deps.extend(ev for k_, ev in b.r.items() if k_ != eng)
        for b in writes:
            if b.w is not None:
                deps.append(b.w)
            deps.extend(b.r.values())
        seen = self.seen[eng]
        need = {}
        for ev in deps:
            if ev.kind == "eng" and ev.key == eng and eng in ("pe", "sp"):
                continue
            if seen.get(ev.key, -1) >= ev.val:
                continue
            if ev.key not in need or need[ev.key].val < ev.val:
                need[ev.key] = ev
        for k, ev in need.items():
            seen[k] = ev.val
            if ev.kind == "eng":
                ev.op.signal = True
            o.waits.append(ev)
        if chan is not None:
            chan.n += 1
            ev = Ev("dma", ("c", chan.idx), chan.n * 16, o)
        else:
            ev = Ev("eng", eng, len(self.ops[eng]), o)
        self.ops[eng].append(o)
        for b in reads:
            b.r[ev.key] = ev
        for b in writes:
            b.w = ev
            b.r = {}
        return ev

    @staticmethod
    def _flat(lst):
        o = []
        for b in lst:
            if isinstance(b, (list, tuple)):
                o.extend(Sched._flat(b))
            else:
                o.append(b)
        return o

    def op(self, eng, fn, reads=(), writes=()):
        return self._record(eng, fn, self._flat(reads), self._flat(writes), None)

    def dma(self, eng, out, in_, chan, reads=(), writes=()):
        return self._record(eng, lambda e: e.dma_start(out=out, in_=in_), self._flat(reads), self._flat(writes), chan)

    def finish(self, final_eng="sp"):
        nc = self.nc
        fin = Op(final_eng, None, None)
        for c in self.chans:
            if c.n:
                fin.waits.append(Ev("dma", ("c", c.idx), c.n * 16, None))
        self.ops[final_eng].append(fin)
        for e in ENGS:
            n = 0
            for o in self.ops[e]:
                if o.signal:
                    n += 1
                    o.sval = n
        handles = {"pe": nc.tensor, "act": nc.scalar, "dve": nc.vector, "pool": nc.gpsimd, "sp": nc.sync}
        chans = self.chans
        esem = self.esem

        def emit(ename):
            h = handles[ename]
            for o in self.ops[ename]:
                for ev in o.waits:
                    if ev.kind == "eng":
                        h.wait_ge(esem[ev.key], ev.op.sval)
                    else:
                        h.wait_ge(chans[ev.key[1]].sem, ev.val)
                if o.fn is None:
                    continue
                ins = o.fn(h)
                if o.chan is not None:
                    ins.then_inc(o.chan.sem, 16)
                elif o.signal:
                    ins.then_inc(esem[ename], 1)

        with nc.Block() as block:
            @block.tensor
            def _(e):
                emit("pe")

            @block.scalar
            def _(e):
                emit("act")

            @block.vector
            def _(e):
                emit("dve")

            @block.gpsimd
            def _(e):
                emit("pool")

            @block.sync
            def _(e):
                emit("sp")
        for cm in reversed(self._stack):
            cm.__exit__(None, None, None)
        self._stack = []


T = 512
D = 1024
DFF = 2816
NH = 8
SCALE = float(192 ** -0.5)
MAGIC = 12582912.0
PI_LO = 3.1415925
NSLOT = 4
import os
DBG = {k: int(v) for k, v in (kv.split('=') for kv in os.environ.get('KDBG', '').split(',') if kv)}
SLOT_ELEMS = 4096


class WMat:
    def __init__(self, S, nc, name, nslab, kc, ncols, per_slab=False):
        assert kc * ncols <= SLOT_ELEMS
        self.scr = nc.dram_tensor("scr_" + name, [nslab, 128, kc * ncols], BF16, kind="Internal").ap()
        self.kc, self.ncols, self.nslab = kc, ncols, nslab
        self.S = S
        if per_slab:
            self.bufs = [Buf("scr_%s_%d" % (name, i)) for i in range(nslab)]
            self.chans = [S.chan() for _ in range(nslab)]
        else:
            b = Buf("scr_" + name)
            c = S.chan()
            self.bufs = [b] * nslab
            self.chans = [c] * nslab
        self.buf = self.bufs[0]
        self.chan = self.chans[0]
        self.srcs = {}
        self.done = [False] * nslab

    def dst(self, i):
        return self.scr[i].rearrange("p (k m) -> p k m", m=self.ncols)

    def conv(self, i, dst_sl, src):
        self.srcs.setdefault(i, []).append((dst_sl, src))


def kview(w2d):
    return w2d.rearrange("(k p) m -> p k m", p=128)


def build_program(nseq, ntile, stage=99):
    nc = bass.Bass("TRN2", target_bir_lowering=False)
    ntok = nseq * ntile * T

    def din(name, shape, dt=F32):
        return nc.dram_tensor(name, list(shape), dt, kind="ExternalInput").ap()

    x = din("x", [ntok, D])
    pos = din("pos", [1, ntok], I32)
    ffn_pre_g = din("ffn_pre_g", [4, D])
    ffn_post_g = din("ffn_post_g", [4, D])
    w_gate = din("ffn_w_gate", [4, D, DFF])
    w_up = din("ffn_w_up", [4, D, DFF])
    w_down = din("ffn_w_down", [4, DFF, D])
    mix_pre_g = din("mix_pre_g", [2, D])
    mix_post_g = din("mix_post_g", [2, D])
    gmlp_w_in = din("gmlp_w_in", [D, 4096])
    ln_g = din("gmlp_ln_g", [2, D])
    ln_b = din("gmlp_ln_b", [2, D])
    w_s = din("gmlp_w_s", [16, 128, 128])
    b_s = din("gmlp_b_s", [1, 2048])
    gmlp_w_out = din("gmlp_w_out", [2048, D])
    kv_norm_g = din("kv_norm_g", [1, D])
    w_dkv = din("w_dkv", [D, 320])
    kv_a_norm_g = din("kv_a_norm_g", [1, 256])
    w_ukv = din("w_ukv", [256, 2048])
    w_dq = din("mla_w_dq", [D, 512])
    q_norm_g = din("mla_q_norm_g", [1, 512])
    w_uq = din("mla_w_uq", [512, 1536])
    w_o = din("mla_w_o", [D, D])
    rope_c = din("rope_c", [64, 2])
    out = nc.dram_tensor("out", [ntok, D], F32, kind="ExternalOutput").ap()

    S = Sched(nc)

    W = {}
    KT = S.sb("KT", [128, NH, ntile * T], BF16)
    KR = S.sb("KR", [65, ntile * T], BF16)
    VS = S.sb("VS", [128, ntile * 4, D], BF16)
    KTb = [[Buf("KT%d_%d" % (h, t)) for t in range(ntile)] for h in range(NH)]
    KRb = [Buf("KR%d" % t) for t in range(ntile)]
    KR1b = Buf("KRones")
    VSb = [Buf("VS%d" % t) for t in range(ntile * 4)]
    hT = S.sb("hT", [128, 8, T], F32)
    hb = [Buf("h%d" % c) for c in range(8)]
    xn = S.sb("xn", [128, 8, T], BF16)
    xnb = [Buf("xn%d" % c) for c in range(8)]
    fA = S.sb("fA", [128, 8, T], F32)
    fb = [Buf("f%d" % c) for c in range(8)]
    AR = S.sb("AR", [128, 32, T], BF16)
    arb = [Buf("ar%d" % c) for c in range(32)]
    slots = [S.sb("wslot%d" % i, [128, SLOT_ELEMS], BF16) for i in range(NSLOT)]
    slot_ch = [S.chan() for _ in range(NSLOT)]
    slot_ch_sw = [S.chan() for _ in range(NSLOT)]
    sqr = [S.sb("sq%d" % i, [128, T], BF16) for i in range(2)]
    rstd = [S.sb("rstd%d" % i, [128, T], F32) for i in range(3)]
    rcol = S.sb("rcol", [128, 4], F32)
    warm = S.sb("warm", [128, 1], F32)
    tmpA = S.sb("tmpA", [128, T], F32)
    tmpB = S.sb("tmpB", [128, T], F32)
    ptile = [S.sb("pt%d" % i, [128, T], BF16) for i in range(3)]
    ident = S.sb("ident", [128, 128], F32)
    ones = {n: S.sb("ones%d" % n, [128, 128], BF16) for n in (1024, 512, 256, 1)}
    tri = S.sb("tri", [128, 128], BF16)
    WmT = S.sb("WmT", [128, 16, 128], BF16)
    Ctab = S.sb("Ctab", [128, 16, 128], F32)
    gT = S.sb("gT", [128, 8, 19], F32)
    epsb = S.sb("epsb", [128, 3], F32)
    ropec = S.sb("ropec", [64, 2], F32)
    cos2 = S.sb("cos2", [64, T], F32)
    sin2 = S.sb("sin2", [64, T], F32)
    ones_row = S.sb("ones_row", [1, 128], F32)
    ones_col = S.sb("ones_col", [128, 1], BF16)
    stat6 = S.sb("stat6", [128, 16, 6], F32)
    mv = S.sb("mv", [128, 4, 2], F32)
    rsd = S.sb("rsd", [128, 2, 4], F32)
    PS = [S.ps("ps%d" % i) for i in range(8)]
    misc_ch = [S.chan() for _ in range(13)]
    io_ch = [S.chan() for _ in range(3)]

    class Alias:
        def __init__(self, t, bufs):
            self.t = t
            self.b = list(bufs)
    ARf = AR.t[:].rearrange("p a b -> p (a b)").bitcast(F32)
    hflat = hT.t[:].rearrange("p a b -> p (a b)")
    fflat = fA.t[:].rearrange("p a b -> p (a b)")
    wst = Alias(ARf[:, 0:2048], arb[0:8])
    lnb_row = Alias(ARf[0:1, 2048:4096], arb[8:16])
    bs_row = Alias(ARf[0:1, 4096:6144], arb[16:24])
    rs_row = Alias(ARf[0:1, 6144:8192], arb[24:32])
    grow = Alias(hflat[0:19, 0:1024], hb[0:2])
    vtok = Alias(fflat[:, 0:2048], fb[0:4])
    posi = Alias(tmpA.t[0:64, :].bitcast(I32), [tmpA.b])
    ang = Alias(tmpB.t[0:64, :], [tmpB.b])
    angq = Alias(rstd[2].t[0:64, :], [rstd[2].b])

    first_tile = {"v": True}

    def PEW():
        return "dve" if first_tile["v"] else "pool"

    rot_state = {"i": 0}

    def rot(banks=(0, 1, 2, 3, 4, 5)):
        i = rot_state["i"]
        rot_state["i"] = i + 1
        return PS[banks[i % len(banks)]]

    ring_state = {"i": 0}

    def load_slab(wm, i):
        k = ring_state["i"] % NSLOT
        ring_state["i"] += 1
        sl = slots[k]
        n = wm.kc * wm.ncols
        view = sl.t[:, 0:n].rearrange("p (k m) -> p k m", m=wm.ncols)
        if not wm.done[i]:
            wm.done[i] = True
            for dst_sl, src in wm.srcs[i]:
                if dst_sl[0] == "hd":
                    _, kk, c0, c1 = dst_sl
                    d = view[:, kk, :].rearrange("p (h d) -> p h d", h=NH)[:, :, c0:c1]
                else:
                    d = view[dst_sl]
                S.dma("pool", d, src, slot_ch_sw[k], writes=[sl.b])
            S.dma("sp", wm.scr[i], sl.t[:, 0:n], wm.chans[i], reads=[sl.b], writes=[wm.bufs[i]])
        else:
            S.dma("sp", sl.t[:, 0:n], wm.scr[i], slot_ch[k], reads=[wm.bufs[i]], writes=[sl.b])
        return view, sl.b

    sq_state = {"i": 0}

    def next_sq():
        i = sq_state["i"] % 2
        sq_state["i"] += 1
        return sqr[i]

    S.op("pool", lambda e: e.memset(ident.t[:], 0.0), writes=[ident.b])
    S.op("pool", lambda e: e.affine_select(out=ident.t[:], in_=ident.t[:], pattern=[[-1, 128]],
                                           compare_op=ALU.not_equal, fill=1.0, base=0, channel_multiplier=1),
         reads=[ident.b], writes=[ident.b])
    for n, tl in ones.items():
        S.op("pool", lambda e, tl=tl, n=n: e.memset(tl.t[:], 1.0 / n), writes=[tl.b])
    S.op("pool", lambda e: e.memset(ones_row.t[:], 1.0), writes=[ones_row.b])
    S.op("pool", lambda e: e.memset(ones_col.t[:], 1.0), writes=[ones_col.b])
    S.op("pool", lambda e: e.memset(epsb.t[:, 0:1], 1e-6), writes=[epsb.b])
    S.op("pool", lambda e: e.memset(epsb.t[:, 1:2], 1e-5), reads=[epsb.b], writes=[epsb.b])
    S.op("pool", lambda e: e.memset(epsb.t[:, 2:3], 4e-6), reads=[epsb.b], writes=[epsb.b])
    S.op("pool", lambda e: e.memset(tri.t[:], 1.0), writes=[tri.b])
    S.op("pool", lambda e: e.affine_select(out=tri.t[:], in_=tri.t[:], pattern=[[1, 128]],
                                           compare_op=ALU.is_ge, fill=0.0, base=0, channel_multiplier=-1),
         reads=[tri.b], writes=[tri.b])
    S.op("pool", lambda e: e.memset(KR.t[64:65, :], 1.0), writes=[KR1b])
    S.op("pool", lambda e: e.memset(grow.t[:], 0.0), writes=[grow.b])
    grows = [(ffn_pre_g, 0, 4, D), (ffn_post_g, 4, 4, D), (mix_pre_g, 8, 2, D), (mix_post_g, 10, 2, D),
             (kv_norm_g, 12, 1, D), (ln_g, 13, 2, D), (ln_b, 15, 2, D), (kv_a_norm_g, 17, 1, 256),
             (q_norm_g, 18, 1, 512)]
    for gi, (src, r0, nr, w) in enumerate(grows):
        S.dma("sp", grow.t[r0:r0 + nr, 0:w], src, misc_ch[gi], reads=[grow.b], writes=[grow.b])
    R_PRE, R_POST, R_MPRE, R_MPOST, R_KV, R_LNG, R_LNB, R_KVA, R_QN = 0, 4, 8, 10, 12, 13, 15, 17, 18
    for c in range(8):
        ps = rot()
        S.op("pe", lambda e, ps=ps, c=c: e.transpose(ps.t[:, 0:19], grow.t[0:19, c * 128:(c + 1) * 128], ident.t[0:19, 0:19]),
             reads=[grow.b, ident.b], writes=[ps.b])
        S.op("dve", lambda e, ps=ps, c=c: e.tensor_copy(out=gT.t[:, c, :], in_=ps.t[:, 0:19]), reads=[ps.b], writes=[gT.b])

    def gcol(r, c):
        return gT.t[:, c, r:r + 1]

    S.dma("sp", ropec.t[:], rope_c, misc_ch[9], writes=[ropec.b])
    S.dma("sp", lnb_row.t[:], ln_b.rearrange("r d -> (r d)").rearrange("(o n) -> o n", o=1), misc_ch[10], writes=[lnb_row.b])
    S.dma("sp", bs_row.t[:], b_s, misc_ch[11], writes=[bs_row.b])
    S.dma("sp", wst.t[:].rearrange("p (g c) -> p g c", g=16), w_s.rearrange("g t c -> t g c"), misc_ch[12],
          writes=[wst.b])
    for g in range(16):
        S.op("pool", lambda e, g=g: e.affine_select(out=wst.t[:, g * 128:(g + 1) * 128], in_=wst.t[:, g * 128:(g + 1) * 128],
                                                    pattern=[[-1, 128]], compare_op=ALU.is_ge, fill=0.0, base=0,
                                                    channel_multiplier=1),
             reads=[wst.b], writes=[wst.b])
    for g in range(16):
        ps = rot()
        S.op("pe", lambda e, ps=ps, g=g: e.transpose(ps.t[:, 0:128], wst.t[:, g * 128:(g + 1) * 128], ident.t[:]),
             reads=[wst.b, ident.b], writes=[ps.b])
        S.op("dve", lambda e, ps=ps, g=g: e.tensor_copy(out=WmT.t[:, g, :], in_=ps.t[:, 0:128]), reads=[ps.b], writes=[WmT.b])
    for q4 in range(4):
        ps = rot()
        S.op("pe", lambda e, ps=ps, q4=q4: e.matmul(ps.t[0:1, :], lhsT=ones_col.t[:, 0:1],
                                                    rhs=WmT.t[:, q4 * 4:(q4 + 1) * 4, :], start=True, stop=True),
             reads=[WmT.b, ones_col.b], writes=[ps.b])
        S.op("dve", lambda e, ps=ps, q4=q4: e.tensor_copy(out=rs_row.t[0:1, q4 * 512:(q4 + 1) * 512], in_=ps.t[0:1, :]),
             reads=[ps.b], writes=[rs_row.b])
    for g in range(16):
        ps = rot()
        S.op("pe", lambda e, ps=ps, g=g: e.matmul(ps.t[:, 0:128], lhsT=lnb_row.t[0:1, g * 128:(g + 1) * 128],
                                                  rhs=rs_row.t[0:1, g * 128:(g + 1) * 128], start=True, stop=False),
             reads=[lnb_row.b, rs_row.b], writes=[ps.b])
        S.op("pe", lambda e, ps=ps, g=g: e.matmul(ps.t[:, 0:128], lhsT=ones_row.t[0:1, :],
                                                  rhs=bs_row.t[0:1, g * 128:(g + 1) * 128], start=False, stop=True),
             reads=[ones_row.b, bs_row.b], writes=[ps.b])
        S.op("dve", lambda e, ps=ps, g=g: e.tensor_copy(out=Ctab.t[:, g, :], in_=ps.t[:, 0:128]), reads=[ps.b], writes=[Ctab.b])

    def mk_gateup(name, w2d, ps_=False):
        wm = WMat(S, nc, name, 6, 8, 512, per_slab=ps_)
        kv = kview(w2d)
        for i in range(6):
            n = 512 if i < 5 else 256
            wm.conv(i, (slice(None), slice(None), slice(0, n)), kv[:, :, i * 512:i * 512 + n])
        return wm

    def mk_down(name, w2d, ps_=False):
        wm = WMat(S, nc, name, 8, 11, 256, per_slab=ps_)
        kv = kview(w2d)
        for mg in range(4):
            for kh in range(2):
                wm.conv(mg * 2 + kh, (slice(None), slice(None), slice(None)),
                        kv[:, kh * 11:(kh + 1) * 11, mg * 256:(mg + 1) * 256])
        return wm

    def mk_plain(name, w2d, kc, ncols, nslab, c0=0):
        wm = WMat(S, nc, name, nslab, kc, ncols)
        kv = kview(w2d)
        for i in range(nslab):
            wm.conv(i, (slice(None), slice(None), slice(None)), kv[:, :, c0 + i * ncols:c0 + (i + 1) * ncols])
        return wm

    def conv_ffn(li):
        W["gate%d" % li] = mk_gateup("gate%d" % li, w_gate[li])
        W["up%d" % li] = mk_gateup("up%d" % li, w_up[li])
        W["down%d" % li] = mk_down("down%d" % li, w_down[li])

    def conv_g1():
        W["win_u"] = mk_plain("win_u", gmlp_w_in, 8, 512, 4, 0)
        W["win_v"] = mk_plain("win_v", gmlp_w_in, 8, 512, 4, 2048)
        W["wout"] = mk_plain("wout", gmlp_w_out, 16, 256, 4)
    def conv_g2():
        conv_ffn(1)
    def conv_g3():
        wm = WMat(S, nc, "dkv", 1, 8, 384)
        kv = kview(w_dkv)
        wm.conv(0, (slice(None), slice(None), slice(0, 320)), kv[:, :, 0:320])
        wm.conv(0, (slice(None), slice(None), slice(320, 352)), kv[:, :, 288:320])
        wm.conv(0, (slice(None), slice(None), slice(352, 384)), kv[:, :, 256:288])
        W["dkv"] = wm
        ukv4 = kview(w_ukv).rearrange("p k (h two d) -> p k h two d", h=NH, two=2)
        for nm_, sel in (("ukv_k", 0), ("ukv_v", 1)):
            wm = WMat(S, nc, nm_, 1, 2, 1024)
            for k in range(2):
                wm.srcs.setdefault(0, []).append((("hd", k, 0, 128), ukv4[:, k, :, sel, :]))
            W[nm_] = wm
    def conv_g4():
        conv_ffn(2)
    def conv_g5():
        W["dq"] = mk_plain("dq", w_dq, 8, 512, 1)
        uq4 = kview(w_uq).rearrange("p k (h e) -> p k h e", h=NH)
        wm = WMat(S, nc, "uq_n", 1, 4, 1024)
        for k in range(4):
            wm.srcs.setdefault(0, []).append((("hd", k, 0, 128), uq4[:, k, :, 0:128]))
        W["uq_n"] = wm
        wm = WMat(S, nc, "uq_r", 1, 4, 1024)
        for k in range(4):
            wm.srcs.setdefault(0, []).append((("hd", k, 0, 64), uq4[:, k, :, 128:192]))
            wm.srcs.setdefault(0, []).append((("hd", k, 64, 96), uq4[:, k, :, 160:192]))
            wm.srcs.setdefault(0, []).append((("hd", k, 96, 128), uq4[:, k, :, 128:160]))
        W["uq_r"] = wm
        W["wo"] = mk_plain("wo", w_o, 8, 512, 2)
    def conv_g6():
        conv_ffn(3)


    for fn_ in (lambda: conv_ffn(0), conv_g1, conv_g2, conv_g3, conv_g4, conv_g5, conv_g6):
        fn_()

    def conv_upto(n):
        pass


    GEN = (0, 1, 2, 3, 4, 6)
    SS = PS[5]
    SS2 = PS[7]
    rs_pre = [rstd[0], rstd[1]]
    rs_post = rstd[2]
    pre_state = {"i": 0}

    def finish_rstd(ps, rs, eps_col):
        S.op("act", lambda e: e.activation(out=rs.t[:], in_=ps.t[:], func=AF.Ln, bias=epsb.t[:, eps_col:eps_col + 1], scale=1.0),
             reads=[ps.b, epsb.b], writes=[rs.b])
        S.op("act", lambda e: e.activation(out=rs.t[:], in_=rs.t[:], func=AF.Exp, scale=-0.5), reads=[rs.b], writes=[rs.b])

    def rms_rstd(chunks, nfeat, rs=None, eps_col=0):
        ps = rot(GEN)
        if rs is None:
            rs = ps
        n = len(chunks)
        for i, (ap, b) in enumerate(chunks):
            sq = next_sq()
            S.op("act", lambda e, sq=sq, ap=ap: e.activation(out=sq.t[:], in_=ap, func=AF.Square),
                 reads=[b], writes=[sq.b])
            S.op("pe", lambda e, sq=sq, ps=ps, i=i: e.matmul(ps.t[:], lhsT=ones[nfeat].t[:], rhs=sq.t[:],
                                                             start=(i == 0), stop=(i == n - 1)),
                 reads=[sq.b, ones[nfeat].b], writes=[ps.b])
        finish_rstd(ps, rs, eps_col)
        return rs

    def prenorm_chunk(c, grow_idx, first, last, src=None):
        if src is None:
            sap, sbuf_ = hT.t[:, c, :], hb[c]
        else:
            sap, sbuf_ = src.t[:], src.b
        S.op("act", lambda e: e.activation(out=xn.t[:, c, :], in_=sap, func=AF.Identity, scale=gcol(grow_idx, c)),
             reads=[sbuf_, gT.b], writes=[xnb[c]])
        sl_ = SQ_SLOTS[c]
        S.op("act", lambda e: e.activation(out=AR.t[:, sl_, :], in_=sap, func=AF.Square), reads=[sbuf_], writes=[arb[sl_]])

        def mm():
            S.op("pe", lambda e: e.matmul(SS2.t[:], lhsT=ones[1024].t[:], rhs=AR.t[:, sl_, :], start=first, stop=last),
                 reads=[arb[sl_], ones[1024].b], writes=[SS2.b])
        deferred.append(mm)

    SQ_SLOTS = (22, 23, 26, 27, 28, 29, 30, 31)
    deferred = []

    def flush_deferred():
        fl = list(deferred)
        del deferred[:]
        for f_ in fl:
            f_()

    def next_rs_pre():
        rs = rs_pre[pre_state["i"] % 2]
        pre_state["i"] += 1
        return rs

    def prenorm_raw(grow_idx):
        rs = next_rs_pre()
        for c in range(8):
            prenorm_chunk(c, grow_idx, c == 0, c == 7)
        deferred.append(lambda: finish_rstd(SS2, rs, 0))
        return rs

    def act_warm_ln():
        S.op("act", lambda e: e.activation(out=warm.t[:], in_=epsb.t[:, 0:1], func=AF.Ln), reads=[epsb.b], writes=[warm.b])

    pend = {"f": None}

    def flush_pend():
        if pend["f"] is not None:
            pend["f"]()
            pend["f"] = None

    def post_cb(grow_post, half):
        on = ones[256] if half else ones[1024]

        def cb(m, ps):
            flush_pend()
            sq = next_sq()
            S.op("act", lambda e: e.activation(out=sq.t[:], in_=ps.t[:], func=AF.Square), reads=[ps.b], writes=[sq.b])
            S.op("dve", lambda e: e.tensor_scalar(out=fA.t[:, m, :], in0=ps.t[:], scalar1=gcol(grow_post, m), scalar2=None,
                                                  op0=ALU.mult), reads=[ps.b, gT.b], writes=[fb[m]])

            def mm():
                S.op("pe", lambda e: e.matmul(SS.t[:], lhsT=on.t[:], rhs=sq.t[:], start=(m == 0), stop=(m == 7)),
                     reads=[sq.b, on.b], writes=[SS.b])
            pend["f"] = mm
        return cb

    def postnorm_residual(half, next_pre):
        flush_pend()
        finish_rstd(SS, SS, 2 if half else 0)
        rs2 = next_rs_pre() if next_pre is not None else None
        for c in range(8):
            S.op("dve", lambda e, c=c: e.tensor_tensor(out=fA.t[:, c, :], in0=fA.t[:, c, :], in1=SS.t[:], op=ALU.mult),
                 reads=[fb[c], SS.b], writes=[fb[c]])
            S.op(PEW(), lambda e, c=c: e.tensor_tensor(out=hT.t[:, c, :], in0=hT.t[:, c, :], in1=fA.t[:, c, :], op=ALU.add),
                 reads=[fb[c], hb[c]], writes=[hb[c]])
            if next_pre is not None:
                prenorm_chunk(c, next_pre, c == 0, c == 7)
        if next_pre is not None:
            deferred.append(lambda: finish_rstd(SS2, rs2, 0))
        return rs2

    def gemm_fm(wm, nk, rhs, mchunks_per_slab, nm_total, cb, ksplit=1, banks=GEN, kouter=0):
        m = 0
        si = 0
        kper = nk // ksplit
        while m < nm_total:
            views = [load_slab(wm, si + j) for j in range(ksplit)]
            si += ksplit
            mm = 0
            if m == 0 and kouter > 1:
                pss = [rot(banks) for _ in range(kouter)]
                for k in range(nk):
                    v, vb = views[k // kper]
                    kk = k % kper
                    for q, ps in enumerate(pss):
                        S.op("pe", lambda e, ps=ps, v=v, kk=kk, q=q, k=k: e.matmul(
                            ps.t[:], lhsT=v[:, kk, q * 128:(q + 1) * 128], rhs=rhs[k][0], start=(k == 0), stop=(k == nk - 1)),
                            reads=[vb, rhs[k][1]], writes=[ps.b])
                flush_deferred()
                for q, ps in enumerate(pss):
                    cb(q, ps)
                m = kouter
                mm = kouter
            while mm < mchunks_per_slab and m < nm_total:
                ps = rot(banks)
                for k in range(nk):
                    v, vb = views[k // kper]
                    kk = k % kper
                    S.op("pe", lambda e, ps=ps, v=v, kk=kk, mm=mm, k=k: e.matmul(
                        ps.t[:], lhsT=v[:, kk, mm * 128:(mm + 1) * 128], rhs=rhs[k][0], start=(k == 0), stop=(k == nk - 1)),
                        reads=[vb, rhs[k][1]], writes=[ps.b])
                cb(m, ps)
                m += 1
                mm += 1

    load_x_ref = [None]

    def ffn(li, rs, next_pre, prefetch=None, mid_hook=None):
        gate, up, down = W["gate%d" % li], W["up%d" % li], W["down%d" % li]

        def evac_a(j, pg):
            flush_deferred()
            tm, tb = fA.t[:, j % 8, :], fb[j % 8]
            S.op("dve", lambda e: e.tensor_tensor(out=tm, in0=pg.t[:], in1=rs.t[:], op=ALU.mult),
                 reads=[pg.b, rs.b], writes=[tb])

        def evac_b(j, pu):
            tm, tb = fA.t[:, j % 8, :], fb[j % 8]
            t2, tb2 = fA.t[:, (j + 4) % 8, :], fb[(j + 4) % 8]
            S.op("dve", lambda e: e.tensor_tensor(out=t2, in0=pu.t[:], in1=rs.t[:], op=ALU.mult),
                 reads=[pu.b, rs.b], writes=[tb2])
            S.op("act", lambda e: e.activation(out=tm, in_=tm, func=AF.Silu), reads=[tb], writes=[tb])
            S.op(PEW(), lambda e: e.tensor_tensor(out=AR.t[:, j, :], in0=tm, in1=t2, op=ALU.mult),
                 reads=[tb, tb2], writes=[arb[j]])

        def evac(j, pg, pu):
            evac_a(j, pg)
            evac_b(j, pu)

        def mmj(ps, v, vb, k, mm):
            S.op("pe", lambda e: e.matmul(ps.t[:], lhsT=v[:, k, mm * 128:(mm + 1) * 128], rhs=xn.t[:, k, :],
                                          start=(k == 0), stop=(k == 7)), reads=[vb, xnb[k]], writes=[ps.b])
        for si in range(6):
            if si == 2 and mid_hook is not None:
                mid_hook()
            gv, gb = load_slab(gate, si)
            uv, ub = load_slab(up, si)
            mm0 = 0
            if si == 0 and DBG.get('kouter', 1):
                pgs = [rot(GEN) for _ in range(3)]
                pus = [rot(GEN) for _ in range(3)]
                for k in range(8):
                    for q in range(3):
                        mmj(pgs[q], gv, gb, k, q)
                        mmj(pus[q], uv, ub, k, q)
                for q in range(3):
                    evac_a(q, pgs[q])
                for q in range(3):
                    evac_b(q, pus[q])
                mm0 = 3
            for mm in range(mm0, 4):
                j = si * 4 + mm
                if j >= 22:
                    break
                pg = rot(GEN)
                pu = rot(GEN)
                for k in range(8):
                    mmj(pg, gv, gb, k, mm)
                for k in range(8):
                    mmj(pu, uv, ub, k, mm)
                evac(j, pg, pu)
        act_warm_ln()
        gemm_fm(down, 22, [(AR.t[:, j, :], arb[j]) for j in range(22)], 2, 8, post_cb(R_POST + li, True), ksplit=2)
        if prefetch is not None:
            load_x_ref[0](prefetch)
        return postnorm_residual(True, next_pre)

    def gmlp(rs, next_pre):
        xr = [(xn.t[:, k, :], xnb[k]) for k in range(8)]
        for fbk in range(4):
            v, vb = load_slab(W["win_v"], fbk)
            pss_ = [rot(GEN) for _ in range(4)]
            if fbk == 0:
                for k in range(8):
                    for n in range(4):
                        S.op("pe", lambda e, k=k, n=n, v=v, pp_=pss_[n]: e.matmul(pp_.t[:], lhsT=xn.t[:, k, n * 128:(n + 1) * 128],
                                                                      rhs=v[:, k, :], start=(k == 0), stop=(k == 7)),
                             reads=[vb, xnb[k]], writes=[pss_[n].b])
                flush_deferred()
                pst = rot(GEN)
                for n in range(4):
                    S.op("pe", lambda e, n=n: e.transpose(pst.t[:, n * 128:(n + 1) * 128], rs.t[:, n * 128:(n + 1) * 128], ident.t[:]),
                         reads=[rs.b, ident.b], writes=[pst.b])
                S.op("dve", lambda e: e.tensor_copy(out=rcol.t[:], in_=pst.t[:].rearrange("p (n t) -> p n t", n=4)[:, :, 0]),
                     reads=[pst.b], writes=[rcol.b])

            for n in range(4):
                ps = pss_[n]
                sl_ = 16 + 4 * n + fbk
                for k in range(8 if fbk > 0 else 0):
                    S.op("pe", lambda e, ps=ps, k=k, n=n, v=v: e.matmul(ps.t[:], lhsT=xn.t[:, k, n * 128:(n + 1) * 128],
                                                                   rhs=v[:, k, :], start=(k == 0), stop=(k == 7)),
                         reads=[vb, xnb[k]], writes=[ps.b])
                S.op("act", lambda e, ps=ps, sl_=sl_, n=n: e.activation(out=AR.t[:, sl_, :], in_=ps.t[:],
                                                                        func=AF.Gelu_apprx_tanh, scale=rcol.t[:, n:n + 1]),
                     reads=[ps.b, rcol.b], writes=[arb[sl_]])
                S.op("dve", lambda e, sl_=sl_, n=n, fbk=fbk: e.bn_stats(out=stat6.t[:, n * 4 + fbk, :], in_=AR.t[:, sl_, :]),
                     reads=[arb[sl_]], writes=[stat6.b])
        for n in range(4):
            S.op("dve", lambda e, n=n: e.bn_aggr(out=mv.t[:, n, :], in_=stat6.t[:, n * 4:(n + 1) * 4, :]),
                 reads=[stat6.b], writes=[mv.b])
        S.op("act", lambda e: e.activation(out=rsd.t[:, 0, :], in_=mv.t[:, :, 1], func=AF.Sqrt, bias=epsb.t[:, 1:2], scale=1.0),
             reads=[mv.b, epsb.b], writes=[rsd.b])
        S.op("dve", lambda e: e.reciprocal(out=rsd.t[:, 0, :], in_=rsd.t[:, 0, :]), reads=[rsd.b], writes=[rsd.b])
        S.op("dve", lambda e: e.scalar_tensor_tensor(out=rsd.t[:, 1, :], in0=mv.t[:, :, 0], scalar=-1.0, in1=rsd.t[:, 0, :],
                                                     op0=ALU.mult, op1=ALU.mult),
             reads=[mv.b, rsd.b], writes=[rsd.b])
        for n in range(4):
            vv_ = AR.t[:, 16 + 4 * n:20 + 4 * n, :].rearrange("p a b -> p (a b)")
            S.op("dve", lambda e, n=n, vv_=vv_: e.tensor_scalar(out=vv_, in0=vv_, scalar1=rsd.t[:, 0, n:n + 1],
                                                                scalar2=rsd.t[:, 1, n:n + 1], op0=ALU.mult, op1=ALU.add),
                 reads=[rsd.b] + [arb[16 + 4 * n + i] for i in range(4)], writes=[arb[16 + 4 * n + i] for i in range(4)])
        def cb_u(m, ps):
            tm = tmpA if m % 2 == 0 else tmpB
            S.op("dve", lambda e: e.tensor_tensor(out=tm.t[:], in0=ps.t[:], in1=rs.t[:], op=ALU.mult),
                 reads=[ps.b, rs.b], writes=[tm.b])
            S.op("act", lambda e: e.activation(out=AR.t[:, m, :], in_=tm.t[:], func=AF.Gelu_apprx_tanh),
                 reads=[tm.b], writes=[arb[m]])
        gemm_fm(W["win_u"], 8, xr, 4, 16, cb_u)
        act_warm_ln()
        wo_ = W["wout"]
        wv = [load_slab(wo_, 0), load_slab(wo_, 1)]
        acc = [PS[q] for q in range(4)]
        sp_banks = (4, 6)

        def spatial(g):
            ps = rot(sp_banks)
            for n in range(4):
                s_ = 16 + 4 * n + g // 4
                S.op("pe", lambda e, n=n, s_=s_: e.matmul(
                    ps.t[:, n * 128:(n + 1) * 128], lhsT=AR.t[:, s_, (g % 4) * 128:(g % 4 + 1) * 128],
                    rhs=WmT.t[:, g, :], start=True, stop=True),
                    reads=[arb[s_], WmT.b], writes=[ps.b])
            tm = tmpA if g % 2 == 0 else tmpB
            S.op("dve", lambda e: e.scalar_tensor_tensor(
                out=tm.t[:].rearrange("p (n t) -> p n t", n=4), in0=ps.t[:].rearrange("p (n t) -> p n t", n=4),
                scalar=gcol(R_LNG + g // 8, g % 8),
                in1=Ctab.t[:, g:g + 1, :].to_broadcast([128, 4, 128]), op0=ALU.mult, op1=ALU.add),
                reads=[ps.b, Ctab.b, gT.b], writes=[tm.b])
            S.op(PEW(), lambda e: e.tensor_tensor(out=AR.t[:, g, :], in0=AR.t[:, g, :], in1=tm.t[:], op=ALU.mult),
                 reads=[arb[g], tm.b], writes=[arb[g]])

        def wout_k(g):
            for q in range(4):
                v, vb = wv[q // 2]
                S.op("pe", lambda e, q=q, v=v: e.matmul(acc[q].t[:], lhsT=v[:, g, (q % 2) * 128:(q % 2 + 1) * 128],
                                                       rhs=AR.t[:, g, :], start=(g == 0), stop=(g == 15)),
                     reads=[vb, arb[g]], writes=[acc[q].b])
        for i in range(16 + 2):
            if i < 16:
                spatial(i)
            if i - 2 >= 0:
                wout_k(i - 2)
        pcb = post_cb(R_MPOST + 0, False)
        for q in range(4):
            pcb(q, acc[q])
        for si in (2, 3):
            v, vb = load_slab(wo_, si)
            for mm in range(2):
                ps = rot(GEN)
                for k in range(16):
                    S.op("pe", lambda e, ps=ps, v=v, k=k, mm=mm: e.matmul(ps.t[:], lhsT=v[:, k, mm * 128:(mm + 1) * 128],
                                                                       rhs=AR.t[:, k, :], start=(k == 0), stop=(k == 15)),
                         reads=[vb, arb[k]], writes=[ps.b])
                pcb(si * 2 + mm, ps)
        return postnorm_residual(False, next_pre)

    def rope_tables(tok0):
        S.dma("pool", ang.t[:], pos[0:1, tok0:tok0 + T].partition_broadcast(64), io_ch[2], writes=[ang.b])
        S.op("dve", lambda e: e.tensor_scalar(out=ang.t[:], in0=ang.t[:], scalar1=ropec.t[:, 0:1], scalar2=None, op0=ALU.mult),
             reads=[ang.b, ropec.b], writes=[ang.b])

        def reduce_sin(dst, shift):
            S.op("dve", lambda e: e.tensor_scalar(out=angq.t[:], in0=ang.t[:], scalar1=float(shift), scalar2=float(1 / (2 * np.pi)),
                                                  op0=ALU.add, op1=ALU.mult), reads=[ang.b], writes=[angq.b])
            S.op("dve", lambda e: e.tensor_scalar(out=angq.t[:], in0=angq.t[:], scalar1=MAGIC, scalar2=None, op0=ALU.add),
                 reads=[angq.b], writes=[angq.b])
            S.op("dve", lambda e: e.tensor_scalar(out=angq.t[:], in0=angq.t[:], scalar1=-MAGIC, scalar2=None, op0=ALU.add),
                 reads=[angq.b], writes=[angq.b])
            S.op("dve", lambda e: e.scalar_tensor_tensor(out=angq.t[:], in0=angq.t[:], scalar=float(-2 * np.pi), in1=ang.t[:],
                                                         op0=ALU.mult, op1=ALU.add), reads=[angq.b, ang.b], writes=[angq.b])
            S.op("dve", lambda e: e.tensor_scalar(out=angq.t[:], in0=angq.t[:], scalar1=float(shift), scalar2=PI_LO,
                                                  op0=ALU.add, op1=ALU.min), reads=[angq.b], writes=[angq.b])
            S.op("dve", lambda e: e.tensor_scalar(out=angq.t[:], in0=angq.t[:], scalar1=-PI_LO, scalar2=None, op0=ALU.max),
                 reads=[angq.b], writes=[angq.b])
            S.op("act", lambda e: e.activation(out=dst.t[:], in_=angq.t[:], func=AF.Sin), reads=[angq.b], writes=[dst.b])
        reduce_sin(sin2, 0.0)
        reduce_sin(cos2, float(np.pi / 2))
        S.op("dve", lambda e: e.tensor_scalar(out=sin2.t[:], in0=sin2.t[:], scalar1=ropec.t[:, 1:2], scalar2=None, op0=ALU.mult),
             reads=[sin2.b, ropec.b], writes=[sin2.b])

    def rope_apply(px, psw, ct, st, dst_ap, dst_bufs, rs=None):
        S.op("dve", lambda e: e.tensor_tensor(out=tmpA.t[0:64, :], in0=px.t[0:64, :], in1=ct.t[:], op=ALU.mult),
             reads=[px.b, ct.b], writes=[tmpA.b])
        S.op("dve", lambda e: e.tensor_tensor(out=tmpB.t[0:64, :], in0=psw.t[0:64, :], in1=st.t[:], op=ALU.mult),
             reads=[psw.b, st.b], writes=[tmpB.b])
        if rs is None:
            S.op(PEW(), lambda e: e.tensor_tensor(out=dst_ap, in0=tmpA.t[0:64, :], in1=tmpB.t[0:64, :], op=ALU.add),
                 reads=[tmpA.b, tmpB.b], writes=dst_bufs)
        else:
            S.op(PEW(), lambda e: e.tensor_tensor(out=tmpA.t[0:64, :], in0=tmpA.t[0:64, :], in1=tmpB.t[0:64, :], op=ALU.add),
                 reads=[tmpA.b, tmpB.b], writes=[tmpA.b])
            S.op("dve", lambda e: e.tensor_tensor(out=dst_ap, in0=tmpA.t[0:64, :], in1=rs.t[0:64, :], op=ALU.mult),
                 reads=[tmpA.b, rs.b], writes=dst_bufs)

    def shared_kv(ti, rs, next_pre):
        t0 = ti * T
        dv, db = load_slab(W["dkv"], 0)
        pcs = [rot(GEN), rot(GEN)]
        px, psw = rot(GEN), rot(GEN)
        for k in range(8):
            for m in range(2):
                S.op("pe", lambda e, k=k, m=m: e.matmul(pcs[m].t[:], lhsT=dv[:, k, m * 128:(m + 1) * 128], rhs=xn.t[:, k, :],
                                                        start=(k == 0), stop=(k == 7)), reads=[db, xnb[k]], writes=[pcs[m].b])
            for pp, c0 in ((px, 256), (psw, 320)):
                S.op("pe", lambda e, pp=pp, k=k, c0=c0: e.matmul(pp.t[0:64, :], lhsT=dv[:, k, c0:c0 + 64], rhs=xn.t[:, k, :],
                                                                start=(k == 0), stop=(k == 7)), reads=[db, xnb[k]], writes=[pp.b])
        flush_deferred()
        for m in range(2):
            S.op("dve", lambda e, m=m: e.tensor_tensor(out=fA.t[:, m, :], in0=pcs[m].t[:], in1=rs.t[:], op=ALU.mult),
                 reads=[pcs[m].b, rs.b], writes=[fb[m]])
        rope_apply(px, psw, cos2, sin2, KR.t[0:64, t0:t0 + T], [KRb[ti]], rs=rs)
        rsc = rms_rstd([(fA.t[:, m, :], fb[m]) for m in range(2)], 256)
        for m in range(2):
            S.op("dve", lambda e, m=m: e.scalar_tensor_tensor(
                out=AR.t[:, 24 + m, :], in0=fA.t[:, m, :], scalar=gcol(R_KVA, m), in1=rsc.t[:], op0=ALU.mult, op1=ALU.mult),
                 reads=[fb[m], rsc.b, gT.b], writes=[arb[24 + m]])
        rs_next = prenorm_raw(next_pre) if next_pre is not None else None
        cn = [(AR.t[:, 24 + m, :], arb[24 + m]) for m in range(2)]

        def cb_k(h, ps):
            S.op("dve", lambda e, h=h, ps=ps: e.tensor_copy(out=KT.t[:, h, t0:t0 + T], in_=ps.t[:]),
                 reads=[ps.b], writes=[KTb[h][ti]])
        gemm_fm(W["ukv_k"], 2, cn, 8, 8, cb_k, kouter=6)
        vv, vb = load_slab(W["ukv_v"], 0)
        for n in range(4):
            for half in range(2):
                ps = rot(GEN)
                for k in range(2):
                    S.op("pe", lambda e, ps=ps, k=k, n=n, half=half: e.matmul(
                        ps.t[:], lhsT=AR.t[:, 24 + k, n * 128:(n + 1) * 128], rhs=vv[:, k, half * 512:(half + 1) * 512],
                        start=(k == 0), stop=(k == 1)), reads=[vb, arb[24 + k]], writes=[ps.b])
                if True:
                    S.op("dve", lambda e, ps=ps, n=n, half=half: e.tensor_copy(
                        out=VS.t[:, ti * 4 + n, half * 512:(half + 1) * 512], in_=ps.t[:]),
                        reads=[ps.b], writes=[VSb[ti * 4 + n]])
                else:
                    S.op("act", lambda e, ps=ps, n=n, half=half: e.activation(
                        out=VS.t[:, ti * 4 + n, half * 512:(half + 1) * 512], in_=ps.t[:], func=AF.Copy),
                        reads=[ps.b], writes=[VSb[ti * 4 + n]])
        flush_deferred()
        return rs_next

    def mla(ti, rs, next_pre):
        xr = [(xn.t[:, k, :], xnb[k]) for k in range(8)]

        def cb_dq(m, ps):
            S.op("dve", lambda e, m=m, ps=ps: e.tensor_tensor(out=fA.t[:, m, :], in0=ps.t[:], in1=rs.t[:], op=ALU.mult),
                 reads=[ps.b, rs.b], writes=[fb[m]])
        gemm_fm(W["dq"], 8, xr, 4, 4, cb_dq, kouter=4)
        rsq = rms_rstd([(fA.t[:, m, :], fb[m]) for m in range(4)], 512)
        for m in range(4):
            S.op("dve", lambda e, m=m: e.scalar_tensor_tensor(
                out=AR.t[:, 24 + m, :], in0=fA.t[:, m, :], scalar=gcol(R_QN, m), in1=rsq.t[:], op0=ALU.mult, op1=ALU.mult),
                 reads=[fb[m], rsq.b, gT.b], writes=[arb[24 + m]])
        cq = [(AR.t[:, 24 + m, :], arb[24 + m]) for m in range(4)]

        def cb_qn(h, ps):
            S.op("act", lambda e, h=h, ps=ps: e.activation(out=AR.t[:, h, :], in_=ps.t[:], func=AF.Copy, scale=SCALE),
                 reads=[ps.b], writes=[arb[h]])
        gemm_fm(W["uq_n"], 4, cq, 8, 8, cb_qn, kouter=6)
        rv, rb = load_slab(W["uq_r"], 0)
        for tb_ in (cos2, sin2):
            S.op("dve", lambda e, tb_=tb_: e.tensor_scalar(out=tb_.t[:], in0=tb_.t[:], scalar1=SCALE, scalar2=None, op0=ALU.mult),
                 reads=[tb_.b], writes=[tb_.b])
        S.op(PEW(), lambda e: e.memset(AR.t[64:65, 8:16, :], 0.0), writes=[arb[8 + h] for h in range(NH)])
        for h in range(NH):
            px, psw = rot(GEN), rot(GEN)
            for pp, c0 in ((px, 0), (psw, 64)):
                for k in range(4):
                    S.op("pe", lambda e, pp=pp, k=k, c0=c0, h=h: e.matmul(
                        pp.t[0:64, :], lhsT=rv[:, k, h * 128 + c0:h * 128 + c0 + 64], rhs=AR.t[:, 24 + k, :],
                        start=(k == 0), stop=(k == 3)), reads=[rb, arb[24 + k]], writes=[pp.b])
            rope_apply(px, psw, cos2, sin2, AR.t[0:64, 8 + h, :], [arb[8 + h]])
        nkb = 4 * (ti + 1)
        SK = 2
        items = [(h, kb) for h in range(NH) for kb in range(nkb)]
        st = {}

        def emit_scores(idx):
            h, kb = items[idx]
            jd = kb - 4 * ti
            q0 = 0 if jd < 0 else jd * 128
            kt_i = kb // 4
            pss = rot((0, 1, 2, 3))
            S.op("pe", lambda e: e.matmul(pss.t[:, q0:T], lhsT=KT.t[:, h, kb * 128:(kb + 1) * 128], rhs=AR.t[:, h, q0:T],
                                          start=True, stop=False),
                 reads=[KTb[h][kt_i], arb[h]], writes=[pss.b])
            S.op("pe", lambda e: e.matmul(pss.t[:, q0:T], lhsT=KR.t[0:65, kb * 128:(kb + 1) * 128], rhs=AR.t[0:65, 8 + h, q0:T],
                                          start=False, stop=True),
                 reads=[KRb[kt_i], KR1b, arb[8 + h]], writes=[pss.b])
            pt = ptile[idx % 3]
            S.op("act", lambda e: e.activation(out=pt.t[:, q0:T], in_=pss.t[:, q0:T], func=AF.Exp),
                 reads=[pss.b], writes=[pt.b])
            if jd >= 0:
                S.op(PEW(), lambda e: e.tensor_tensor(out=pt.t[:, q0:q0 + 128], in0=pt.t[:, q0:q0 + 128],
                                                       in1=tri.t[:], op=ALU.mult),
                     reads=[pt.b, tri.b], writes=[pt.b])
            st[idx] = (pt, q0)

        def emit_pv(idx):
            h, kb = items[idx]
            pt, q0 = st.pop(idx)
            po = PS[4 + 2 * (h % 2)]
            pd = PS[5 + 2 * (h % 2)]
            S.op("pe", lambda e: e.matmul(po.t[:, q0:T], lhsT=VS.t[:, kb, h * 128:(h + 1) * 128], rhs=pt.t[:, q0:T],
                                          start=(kb == 0), stop=(kb == nkb - 1)),
                 reads=[VSb[kb], pt.b], writes=[po.b])
            S.op("pe", lambda e: e.matmul(pd.t[:, q0:T], lhsT=ones[1].t[:], rhs=pt.t[:, q0:T],
                                          start=(kb == 0), stop=(kb == nkb - 1)),
                 reads=[ones[1].b, pt.b], writes=[pd.b])
            if kb == nkb - 1:
                S.op("dve", lambda e: e.reciprocal(out=tmpA.t[:], in_=pd.t[:]), reads=[pd.b], writes=[tmpA.b])
                S.op("dve", lambda e: e.tensor_tensor(out=AR.t[:, 16 + h, :], in0=po.t[:], in1=tmpA.t[:], op=ALU.mult),
                     reads=[po.b, tmpA.b], writes=[arb[16 + h]])

        for i in range(len(items) + SK):
            if i < len(items):
                emit_scores(i)
            if i - SK >= 0:
                emit_pv(i - SK)

        gemm_fm(W["wo"], 8, [(AR.t[:, 16 + h, :], arb[16 + h]) for h in range(NH)], 4, 8, post_cb(R_MPOST + 1, False),
                banks=(0, 1, 2, 3), kouter=4)
        return postnorm_residual(False, next_pre)

    fx = fA.t[:].rearrange("p c t -> p (c t)").rearrange("p (n d) -> p n d", n=4)
    xin = ARf[:, 0:4096].rearrange("p (n d) -> p n d", n=4)
    xin_b = arb[0:16]
    tiles = [(s_, ti) for s_ in range(nseq) for ti in range(ntile)]

    def load_x(idx):
        s_, ti = tiles[idx]
        tok0 = (s_ * ntile + ti) * T
        S.dma("sp", xin, x[tok0:tok0 + T, :].rearrange("(n p) d -> p n d", p=128), io_ch[0], writes=xin_b)

    load_x_ref[0] = load_x
    load_x(0)
    for idx, (s, ti) in enumerate(tiles):
        if True:
            tok0 = (s * ntile + ti) * T
            first_tile["v"] = (idx == 0)
            for c in range(8):
                ps = rot()
                for n in range(4):
                    S.op("pe", lambda e, ps=ps, n=n, c=c: e.transpose(ps.t[:, n * 128:(n + 1) * 128],
                                                                   xin[:, n, c * 128:(c + 1) * 128], ident.t[:]),
                         reads=xin_b + [ident.b], writes=[ps.b])
                S.op("dve", lambda e, ps=ps, c=c: e.tensor_copy(out=hT.t[:, c, :], in_=ps.t[:]), reads=[ps.b], writes=[hb[c]])
                prenorm_chunk(c, R_PRE + 0, c == 0, c == 7, src=ps)
            rs = next_rs_pre()
            deferred.append(lambda rs=rs: finish_rstd(SS2, rs, 0))
            if stage >= 1:
                rs = ffn(0, rs, R_MPRE + 0, mid_hook=(lambda tok0=tok0: rope_tables(tok0)) if stage >= 4 else None)
            if stage >= 2:
                rs = gmlp(rs, R_PRE + 1)
            if stage >= 3:
                rs = ffn(1, rs, R_KV)
            if stage >= 4:
                rs = shared_kv(ti, rs, R_PRE + 2)
            if stage >= 5:
                rs = ffn(2, rs, R_MPRE + 1)
            if stage >= 6:
                rs = mla(ti, rs, R_PRE + 3)
            if stage >= 7:
                ffn(3, rs, None, prefetch=(idx + 1) if idx + 1 < len(tiles) else None)
            elif idx + 1 < len(tiles):
                load_x(idx + 1)
            flush_deferred()
            for c in range(8):
                ps = rot()
                for n in range(4):
                    S.op("pe", lambda e, ps=ps, n=n, c=c: e.transpose(ps.t[:, n * 128:(n + 1) * 128],
                                                                   hT.t[:, c, n * 128:(n + 1) * 128], ident.t[:]),
                         reads=[hb[c], ident.b], writes=[ps.b])
                S.op("dve", lambda e, ps=ps, c=c: e.tensor_copy(out=fx[:, :, c * 128:(c + 1) * 128],
                                                                in_=ps.t[:].rearrange("p (n d) -> p n d", n=4)),
                     reads=[ps.b], writes=fb)
            S.dma("pool", out[tok0:tok0 + T, :].rearrange("(n p) d -> p n d", p=128), fx, io_ch[1], reads=fb)
    build_program.sbuf_left = nc.sbuf_bytes_remaining
    S.finish(final_eng="pool")
    return nc


def rope_consts():
    inv = (10000.0 ** (-np.arange(0, 64, 2, dtype=np.float32) / 64)).astype(np.float32)
    c = np.zeros((64, 2), np.float32)
    c[:, 0] = np.concatenate([inv, inv])
    c[:32, 1] = -1.0
    c[32:, 1] = 1.0
    return c


def make_in_maps(inputs, ncores, nseq, ntile):
    f = lambda a: np.ascontiguousarray(np.asarray(a))
    S_ = ntile * T
    shared = {
        "ffn_pre_g": f(inputs["ffn_pre_g"]).reshape(4, D),
        "ffn_post_g": f(inputs["ffn_post_g"]).reshape(4, D),
        "ffn_w_gate": f(inputs["ffn_w_gate"]).reshape(4, D, DFF),
        "ffn_w_up": f(inputs["ffn_w_up"]).reshape(4, D, DFF),
        "ffn_w_down": f(inputs["ffn_w_down"]).reshape(4, DFF, D),
        "mix_pre_g": f(inputs["mix_pre_g"]),
        "mix_post_g": f(inputs["mix_post_g"]),
        "gmlp_w_in": f(inputs["gmlp_w_in"]).reshape(D, 4096),
        "gmlp_ln_g": f(inputs["gmlp_ln_g"]).reshape(2, D),
        "gmlp_ln_b": f(inputs["gmlp_ln_b"]).reshape(2, D),
        "gmlp_w_s": f(inputs["gmlp_w_s"]).reshape(16, 128, 128),
        "gmlp_b_s": f(inputs["gmlp_b_s"]).reshape(1, 2048),
        "gmlp_w_out": f(inputs["gmlp_w_out"]).reshape(2048, D),
        "kv_norm_g": f(inputs["kv_norm_g"]).reshape(1, D),
        "w_dkv": f(inputs["w_dkv"]),
        "kv_a_norm_g": f(inputs["kv_a_norm_g"]).reshape(1, 256),
        "w_ukv": f(inputs["w_ukv"]),
        "mla_w_dq": f(inputs["mla_w_dq"]).reshape(D, 512),
        "mla_q_norm_g": f(inputs["mla_q_norm_g"]).reshape(1, 512),
        "mla_w_uq": f(inputs["mla_w_uq"]).reshape(512, 1536),
        "mla_w_o": f(inputs["mla_w_o"]).reshape(D, D),
        "rope_c": rope_consts(),
    }
    xs = np.asarray(inputs["x"])
    ps = np.asarray(inputs["positions"])
    maps = []
    for c in range(ncores):
        m = dict(shared)
        m["x"] = np.ascontiguousarray(xs[c * nseq:(c + 1) * nseq, :S_, :]).reshape(nseq * S_, D)
        m["pos"] = np.ascontiguousarray(ps[c * nseq:(c + 1) * nseq, :S_]).reshape(1, nseq * S_).astype(np.int32)
        maps.append(m)
    return maps


_NC_CACHE = {}


def run(inputs, ncores=8, nseq=2, ntile=4, stage=99):
    key = (nseq, ntile, stage)
    if key not in _NC_CACHE:
        _NC_CACHE[key] = build_program(nseq, ntile, stage)
    nc = _NC_CACHE[key]
    maps = make_in_maps(inputs, ncores, nseq, ntile)
    res = run_bass_kernel_spmd(nc, maps, core_ids=list(range(ncores)))
    outs = [np.asarray(r["out"]).reshape(nseq, ntile * T, D) for r in res.results]
    return np.concatenate(outs, axis=0)


def kernel(**inputs):
    return run(inputs).astype(np.float32)
```

```python
import numpy as np
import concourse.bass as bass
import concourse.mybir as mybir
from concourse.bass_utils import run_bass_kernel_spmd

F32 = mybir.dt.float32
BF16 = mybir.dt.bfloat16
I32 = mybir.dt.int32
ALU = mybir.AluOpType
AF = mybir.ActivationFunctionType
AX = mybir.AxisListType


class Buf:
    __slots__ = ("name", "w", "r", "excl")

    def __init__(self, name, excl=False):
        self.name = name
        self.w = None
        self.r = {}
        self.excl = excl


class Tile_:
    __slots__ = ("t", "b")

    def __init__(self, t, name):
        self.t = t
        self.b = Buf(name)


class Chan:
    __slots__ = ("sem", "n", "idx")

    def __init__(self, sem, idx):
        self.sem = sem
        self.n = 0
        self.idx = idx


class Ev:
    __slots__ = ("kind", "key", "val", "op")

    def __init__(self, kind, key, val, op):
        self.kind = kind
        self.key = key
        self.val = val
        self.op = op


class Op:
    __slots__ = ("eng", "fn", "waits", "signal", "chan", "sval")

    def __init__(self, eng, fn, chan):
        self.eng = eng
        self.fn = fn
        self.waits = []
        self.signal = False
        self.chan = chan
        self.sval = 0


ENGS = ("pe", "act", "dve", "pool", "sp")


class Sched:
    def __init__(self, nc):
        self.nc = nc
        self.ops = {e: [] for e in ENGS}
        self.seen = {e: {} for e in ENGS}
        self.chans = []
        self._stack = []
        self.esem = {}
        for e in ENGS:
            cm = nc.semaphore("s_" + e)
            self.esem[e] = cm.__enter__()
            self._stack.append(cm)
        self.nps = 0

    def sb(self, name, shape, dt):
        cm = self.nc.sbuf_tensor(name, list(shape), dt)
        t = cm.__enter__()
        self._stack.append(cm)
        return Tile_(t, name)

    def ps(self, name, shape=(128, 512), dt=F32):
        cm = self.nc.psum_tensor(name, list(shape), dt)
        t = cm.__enter__()
        self._stack.append(cm)
        tl = Tile_(t, name)
        tl.b.excl = True
        return tl

    def chan(self):
        cm = self.nc.semaphore("c%d" % len(self.chans))
        s = cm.__enter__()
        self._stack.append(cm)
        c = Chan(s, len(self.chans))
        self.chans.append(c)
        return c

    def _record(self, eng, fn, reads, writes, chan):
        o = Op(eng, fn, chan)
        deps = []
        for b in reads:
            if b.w is not None:
                deps.append(b.w)
            if b.excl:
                deps.extend(ev for k_, ev in b.r.items() if k_ != eng)
        for b in writes:
            if b.w is not None:
                deps.append(b.w)
            deps.extend(b.r.values())
        seen = self.seen[eng]
        need = {}
        for ev in deps:
            if ev.kind == "eng" and ev.key == eng and eng in ("pe", "sp"):
                continue
            if seen.get(ev.key, -1) >= ev.val:
                continue
            if ev.key not in need or need[ev.key].val < ev.val:
                need[ev.key] = ev
        for k, ev in need.items():
            seen[k] = ev.val
            if ev.kind == "eng":
                ev.op.signal = True
            o.waits.append(ev)
        if chan is not None:
            chan.n += 1
            ev = Ev("dma", ("c", chan.idx), chan.n * 16, o)
        else:
            ev = Ev("eng", eng, len(self.ops[eng]), o)
        self.ops[eng].append(o)
        for b in reads:
            b.r[ev.key] = ev
        for b in writes:
            b.w = ev
            b.r = {}
        return ev

    @staticmethod
    def _flat(lst):
        o = []
        for b in lst:
            if isinstance(b, (list, tuple)):
                o.extend(Sched._flat(b))
            else:
                o.append(b)
        return o

    def op(self, eng, fn, reads=(), writes=()):
        return self._record(eng, fn, self._flat(reads), self._flat(writes), None)

    def dma(self, eng, out, in_, chan, reads=(), writes=()):
        return self._record(eng, lambda e: e.dma_start(out=out, in_=in_), self._flat(reads), self._flat(writes), chan)

    def finish(self, final_eng="sp"):
        nc = self.nc
        fin = Op(final_eng, None, None)
        for c in self.chans:
            if c.n:
                fin.waits.append(Ev("dma", ("c", c.idx), c.n * 16, None))
        self.ops[final_eng].append(fin)
        for e in ENGS:
            n = 0
            for o in self.ops[e]:
                if o.signal:
                    n += 1
                    o.sval = n
        handles = {"pe": nc.tensor, "act": nc.scalar, "dve": nc.vector, "pool": nc.gpsimd, "sp": nc.sync}
        chans = self.chans
        esem = self.esem

        def emit(ename):
            h = handles[ename]
            for o in self.ops[ename]:
                for ev in o.waits:
                    if ev.kind == "eng":
                        h.wait_ge(esem[ev.key], ev.op.sval)
                    else:
                        h.wait_ge(chans[ev.key[1]].sem, ev.val)
                if o.fn is None:
                    continue
                ins = o.fn(h)
                if o.chan is not None:
                    ins.then_inc(o.chan.sem, 16)
                elif o.signal:
                    ins.then_inc(esem[ename], 1)

        with nc.Block() as block:
            @block.tensor
            def _(e):
                emit("pe")

            @block.scalar
            def _(e):
                emit("act")

            @block.vector
            def _(e):
                emit("dve")

            @block.gpsimd
            def _(e):
                emit("pool")

            @block.sync
            def _(e):
                emit("sp")
        for cm in reversed(self._stack):
            cm.__exit__(None, None, None)
        self._stack = []


T = 512
D = 1024
DFF = 2816
NH = 8
SCALE = float(192 ** -0.5)
MAGIC = 12582912.0
PI_LO = 3.1415925
NSLOT = 4
import os
DBG = {k: int(v) for k, v in (kv.split('=') for kv in os.environ.get('KDBG', '').split(',') if kv)}
SLOT_ELEMS = 4096


class WMat:
    def __init__(self, S, nc, name, nslab, kc, ncols, per_slab=False):
        assert kc * ncols <= SLOT_ELEMS
        self.scr = nc.dram_tensor("scr_" + name, [nslab, 128, kc * ncols], BF16, kind="Internal").ap()
        self.kc, self.ncols, self.nslab = kc, ncols, nslab
        self.S = S
        if per_slab:
            self.bufs = [Buf("scr_%s_%d" % (name, i)) for i in range(nslab)]
            self.chans = [S.chan() for _ in range(nslab)]
        else:
            b = Buf("scr_" + name)
            c = S.chan()
            self.bufs = [b] * nslab
            self.chans = [c] * nslab
        self.buf = self.bufs[0]
        self.chan = self.chans[0]
        self.srcs = {}
        self.done = [False] * nslab

    def dst(self, i):
        return self.scr[i].rearrange("p (k m) -> p k m", m=self.ncols)

    def conv(self, i, dst_sl, src):
        self.srcs.setdefault(i, []).append((dst_sl, src))


def kview(w2d):
    return w2d.rearrange("(k p) m -> p k m", p=128)


def build_program(nseq, ntile, stage=99):
    nc = bass.Bass("TRN2", target_bir_lowering=False)
    ntok = nseq * ntile * T

    def din(name, shape, dt=F32):
        return nc.dram_tensor(name, list(shape), dt, kind="ExternalInput").ap()

    x = din("x", [ntok, D])
    pos = din("pos", [1, ntok], I32)
    ffn_pre_g = din("ffn_pre_g", [4, D])
    ffn_post_g = din("ffn_post_g", [4, D])
    w_gate = din("ffn_w_gate", [4, D, DFF])
    w_up = din("ffn_w_up", [4, D, DFF])
    w_down = din("ffn_w_down", [4, DFF, D])
    mix_pre_g = din("mix_pre_g", [2, D])
    mix_post_g = din("mix_post_g", [2, D])
    gmlp_w_in = din("gmlp_w_in", [D, 4096])
    ln_g = din("gmlp_ln_g", [2, D])
    ln_b = din("gmlp_ln_b", [2, D])
    w_s = din("gmlp_w_s", [16, 128, 128])
    b_s = din("gmlp_b_s", [1, 2048])
    gmlp_w_out = din("gmlp_w_out", [2048, D])
    kv_norm_g = din("kv_norm_g", [1, D])
    w_dkv = din("w_dkv", [D, 320])
    kv_a_norm_g = din("kv_a_norm_g", [1, 256])
    w_ukv = din("w_ukv", [256, 2048])
    w_dq = din("mla_w_dq", [D, 512])
    q_norm_g = din("mla_q_norm_g", [1, 512])
    w_uq = din("mla_w_uq", [512, 1536])
    w_o = din("mla_w_o", [D, D])
    rope_c = din("rope_c", [64, 2])
    out = nc.dram_tensor("out", [ntok, D], F32, kind="ExternalOutput").ap()

    S = Sched(nc)

    W = {}
    KT = S.sb("KT", [128, NH, ntile * T], BF16)
    KR = S.sb("KR", [65, ntile * T], BF16)
    VS = S.sb("VS", [128, ntile * 4, D], BF16)
    KTb = [[Buf("KT%d_%d" % (h, t)) for t in range(ntile)] for h in range(NH)]
    KRb = [Buf("KR%d" % t) for t in range(ntile)]
    KR1b = Buf("KRones")
    VSb = [Buf("VS%d" % t) for t in range(ntile * 4)]
    hT = S.sb("hT", [128, 8, T], F32)
    hb = [Buf("h%d" % c) for c in range(8)]
    xn = S.sb("xn", [128, 8, T], BF16)
    xnb = [Buf("xn%d" % c) for c in range(8)]
    fA = S.sb("fA", [128, 8, T], F32)
    fb = [Buf("f%d" % c) for c in range(8)]
    AR = S.sb("AR", [128, 32, T], BF16)
    arb = [Buf("ar%d" % c) for c in range(32)]
    slots = [S.sb("wslot%d" % i, [128, SLOT_ELEMS], BF16) for i in range(NSLOT)]
    slot_ch = [S.chan() for _ in range(NSLOT)]
    slot_ch_sw = [S.chan() for _ in range(NSLOT)]
    sqr = [S.sb("sq%d" % i, [128, T], BF16) for i in range(2)]
    rstd = [S.sb("rstd%d" % i, [128, T], F32) for i in range(3)]
    rcol = S.sb("rcol", [128, 4], F32)
    warm = S.sb("warm", [128, 1], F32)
    tmpA = S.sb("tmpA", [128, T], F32)
    tmpB = S.sb("tmpB", [128, T], F32)
    ptile = [S.sb("pt%d" % i, [128, T], BF16) for i in range(3)]
    ident = S.sb("ident", [128, 128], F32)
    ones = {n: S.sb("ones%d" % n, [128, 128], BF16) for n in (1024, 512, 256, 1)}
    tri = S.sb("tri", [128, 128], BF16)
    WmT = S.sb("WmT", [128, 16, 128], BF16)
    Ctab = S.sb("Ctab", [128, 16, 128], F32)
    gT = S.sb("gT", [128, 8, 19], F32)
    epsb = S.sb("epsb", [128, 3], F32)
    ropec = S.sb("ropec", [64, 2], F32)
    cos2 = S.sb("cos2", [64, T], F32)
    sin2 = S.sb("sin2", [64, T], F32)
    ones_row = S.sb("ones_row", [1, 128], F32)
    ones_col = S.sb("ones_col", [128, 1], BF16)
    stat6 = S.sb("stat6", [128, 16, 6], F32)
    mv = S.sb("mv", [128, 4, 2], F32)
    rsd = S.sb("rsd", [128, 2, 4], F32)
    PS = [S.ps("ps%d" % i) for i in range(8)]
    misc_ch = [S.chan() for _ in range(13)]
    io_ch = [S.chan() for _ in range(3)]

    class Alias:
        def __init__(self, t, bufs):
            self.t = t
            self.b = list(bufs)
    ARf = AR.t[:].rearrange("p a b -> p (a b)").bitcast(F32)
    hflat = hT.t[:].rearrange("p a b -> p (a b)")
    fflat = fA.t[:].rearrange("p a b -> p (a b)")
    wst = Alias(ARf[:, 0:2048], arb[0:8])
    lnb_row = Alias(ARf[0:1, 2048:4096], arb[8:16])
    bs_row = Alias(ARf[0:1, 4096:6144], arb[16:24])
    rs_row = Alias(ARf[0:1, 6144:8192], arb[24:32])
    grow = Alias(hflat[0:19, 0:1024], hb[0:2])
    vtok = Alias(fflat[:, 0:2048], fb[0:4])
    posi = Alias(tmpA.t[0:64, :].bitcast(I32), [tmpA.b])
    ang = Alias(tmpB.t[0:64, :], [tmpB.b])
    angq = Alias(rstd[2].t[0:64, :], [rstd[2].b])

    first_tile = {"v": True}

    def PEW():
        return "dve" if first_tile["v"] else "pool"

    rot_state = {"i": 0}

    def rot(banks=(0, 1, 2, 3, 4, 5)):
        i = rot_state["i"]
        rot_state["i"] = i + 1
        return PS[banks[i % len(banks)]]

    ring_state = {"i": 0}

    def load_slab(wm, i):
        k = ring_state["i"] % NSLOT
        ring_state["i"] += 1
        sl = slots[k]
        n = wm.kc * wm.ncols
        view = sl.t[:, 0:n].rearrange("p (k m) -> p k m", m=wm.ncols)
        if not wm.done[i]:
            wm.done[i] = True
            for dst_sl, src in wm.srcs[i]:
                if dst_sl[0] == "hd":
                    _, kk, c0, c1 = dst_sl
                    d = view[:, kk, :].rearrange("p (h d) -> p h d", h=NH)[:, :, c0:c1]
                else:
                    d = view[dst_sl]
                S.dma("pool", d, src, slot_ch_sw[k], writes=[sl.b])
            S.dma("sp", wm.scr[i], sl.t[:, 0:n], wm.chans[i], reads=[sl.b], writes=[wm.bufs[i]])
        else:
            S.dma("sp", sl.t[:, 0:n], wm.scr[i], slot_ch[k], reads=[wm.bufs[i]], writes=[sl.b])
        return view, sl.b

    sq_state = {"i": 0}

    def next_sq():
        i = sq_state["i"] % 2
        sq_state["i"] += 1
        return sqr[i]

    S.op("pool", lambda e: e.memset(ident.t[:], 0.0), writes=[ident.b])
    S.op("pool", lambda e: e.affine_select(out=ident.t[:], in_=ident.t[:], pattern=[[-1, 128]],
                                           compare_op=ALU.not_equal, fill=1.0, base=0, channel_multiplier=1),
         reads=[ident.b], writes=[ident.b])
    for n, tl in ones.items():
        S.op("pool", lambda e, tl=tl, n=n: e.memset(tl.t[:], 1.0 / n), writes=[tl.b])
    S.op("pool", lambda e: e.memset(ones_row.t[:], 1.0), writes=[ones_row.b])
    S.op("pool", lambda e: e.memset(ones_col.t[:], 1.0), writes=[ones_col.b])
    S.op("pool", lambda e: e.memset(epsb.t[:, 0:1], 1e-6), writes=[epsb.b])
    S.op("pool", lambda e: e.memset(epsb.t[:, 1:2], 1e-5), reads=[epsb.b], writes=[epsb.b])
    S.op("pool", lambda e: e.memset(epsb.t[:, 2:3], 4e-6), reads=[epsb.b], writes=[epsb.b])
    S.op("pool", lambda e: e.memset(tri.t[:], 1.0), writes=[tri.b])
    S.op("pool", lambda e: e.affine_select(out=tri.t[:], in_=tri.t[:], pattern=[[1, 128]],
                                           compare_op=ALU.is_ge, fill=0.0, base=0, channel_multiplier=-1),
         reads=[tri.b], writes=[tri.b])
    S.op("pool", lambda e: e.memset(KR.t[64:65, :], 1.0), writes=[KR1b])
    S.op("pool", lambda e: e.memset(grow.t[:], 0.0), writes=[grow.b])
    grows = [(ffn_pre_g, 0, 4, D), (ffn_post_g, 4, 4, D), (mix_pre_g, 8, 2, D), (mix_post_g, 10, 2, D),
             (kv_norm_g, 12, 1, D), (ln_g, 13, 2, D), (ln_b, 15, 2, D), (kv_a_norm_g, 17, 1, 256),
             (q_norm_g, 18, 1, 512)]
    for gi, (src, r0, nr, w) in enumerate(grows):
        S.dma("sp", grow.t[r0:r0 + nr, 0:w], src, misc_ch[gi], reads=[grow.b], writes=[grow.b])
    R_PRE, R_POST, R_MPRE, R_MPOST, R_KV, R_LNG, R_LNB, R_KVA, R_QN = 0, 4, 8, 10, 12, 13, 15, 17, 18
    for c in range(8):
        ps = rot()
        S.op("pe", lambda e, ps=ps, c=c: e.transpose(ps.t[:, 0:19], grow.t[0:19, c * 128:(c + 1) * 128], ident.t[0:19, 0:19]),
             reads=[grow.b, ident.b], writes=[ps.b])
        S.op("dve", lambda e, ps=ps, c=c: e.tensor_copy(out=gT.t[:, c, :], in_=ps.t[:, 0:19]), reads=[ps.b], writes=[gT.b])

    def gcol(r, c):
        return gT.t[:, c, r:r + 1]

    S.dma("sp", ropec.t[:], rope_c, misc_ch[9], writes=[ropec.b])
    S.dma("sp", lnb_row.t[:], ln_b.rearrange("r d -> (r d)").rearrange("(o n) -> o n", o=1), misc_ch[10], writes=[lnb_row.b])
    S.dma("sp", bs_row.t[:], b_s, misc_ch[11], writes=[bs_row.b])
    S.dma("sp", wst.t[:].rearrange("p (g c) -> p g c", g=16), w_s.rearrange("g t c -> t g c"), misc_ch[12],
          writes=[wst.b])
    for g in range(16):
        S.op("pool", lambda e, g=g: e.affine_select(out=wst.t[:, g * 128:(g + 1) * 128], in_=wst.t[:, g * 128:(g + 1) * 128],
                                                    pattern=[[-1, 128]], compare_op=ALU.is_ge, fill=0.0, base=0,
                                                    channel_multiplier=1),
             reads=[wst.b], writes=[wst.b])
    for g in range(16):
        ps = rot()
        S.op("pe", lambda e, ps=ps, g=g: e.transpose(ps.t[:, 0:128], wst.t[:, g * 128:(g + 1) * 128], ident.t[:]),
             reads=[wst.b, ident.b], writes=[ps.b])
        S.op("dve", lambda e, ps=ps, g=g: e.tensor_copy(out=WmT.t[:, g, :], in_=ps.t[:, 0:128]), reads=[ps.b], writes=[WmT.b])
    for q4 in range(4):
        ps = rot()
        S.op("pe", lambda e, ps=ps, q4=q4: e.matmul(ps.t[0:1, :], lhsT=ones_col.t[:, 0:1],
                                                    rhs=WmT.t[:, q4 * 4:(q4 + 1) * 4, :], start=True, stop=True),
             reads=[WmT.b, ones_col.b], writes=[ps.b])
        S.op("dve", lambda e, ps=ps, q4=q4: e.tensor_copy(out=rs_row.t[0:1, q4 * 512:(q4 + 1) * 512], in_=ps.t[0:1, :]),
             reads=[ps.b], writes=[rs_row.b])
    for g in range(16):
        ps = rot()
        S.op("pe", lambda e, ps=ps, g=g: e.matmul(ps.t[:, 0:128], lhsT=lnb_row.t[0:1, g * 128:(g + 1) * 128],
                                                  rhs=rs_row.t[0:1, g * 128:(g + 1) * 128], start=True, stop=False),
             reads=[lnb_row.b, rs_row.b], writes=[ps.b])
        S.op("pe", lambda e, ps=ps, g=g: e.matmul(ps.t[:, 0:128], lhsT=ones_row.t[0:1, :],
                                                  rhs=bs_row.t[0:1, g * 128:(g + 1) * 128], start=False, stop=True),
             reads=[ones_row.b, bs_row.b], writes=[ps.b])
        S.op("dve", lambda e, ps=ps, g=g: e.tensor_copy(out=Ctab.t[:, g, :], in_=ps.t[:, 0:128]), reads=[ps.b], writes=[Ctab.b])

    def mk_gateup(name, w2d, ps_=False):
        wm = WMat(S, nc, name, 6, 8, 512, per_slab=ps_)
        kv = kview(w2d)
        for i in range(6):
            n = 512 if i < 5 else 256
            wm.conv(i, (slice(None), slice(None), slice(0, n)), kv[:, :, i * 512:i * 512 + n])
        return wm

    def mk_down(name, w2d, ps_=False):
        wm = WMat(S, nc, name, 8, 11, 256, per_slab=ps_)
        kv = kview(w2d)
        for mg in range(4):
            for kh in range(2):
                wm.conv(mg * 2 + kh, (slice(None), slice(None), slice(None)),
                        kv[:, kh * 11:(kh + 1) * 11, mg * 256:(mg + 1) * 256])
        return wm

    def mk_plain(name, w2d, kc, ncols, nslab, c0=0):
        wm = WMat(S, nc, name, nslab, kc, ncols)
        kv = kview(w2d)
        for i in range(nslab):
            wm.conv(i, (slice(None), slice(None), slice(None)), kv[:, :, c0 + i * ncols:c0 + (i + 1) * ncols])
        return wm

    def conv_ffn(li):
        W["gate%d" % li] = mk_gateup("gate%d" % li, w_gate[li])
        W["up%d" % li] = mk_gateup("up%d" % li, w_up[li])
        W["down%d" % li] = mk_down("down%d" % li, w_down[li])

    def conv_g1():
        W["win_u"] = mk_plain("win_u", gmlp_w_in, 8, 512, 4, 0)
        W["win_v"] = mk_plain("win_v", gmlp_w_in, 8, 512, 4, 2048)
        W["wout"] = mk_plain("wout", gmlp_w_out, 16, 256, 4)
    def conv_g2():
        conv_ffn(1)
    def conv_g3():
        wm = WMat(S, nc, "dkv", 1, 8, 384)
        kv = kview(w_dkv)
        wm.conv(0, (slice(None), slice(None), slice(0, 320)), kv[:, :, 0:320])
        wm.conv(0, (slice(None), slice(None), slice(320, 352)), kv[:, :, 288:320])
        wm.conv(0, (slice(None), slice(None), slice(352, 384)), kv[:, :, 256:288])
        W["dkv"] = wm
        ukv4 = kview(w_ukv).rearrange("p k (h two d) -> p k h two d", h=NH, two=2)
        for nm_, sel in (("ukv_k", 0), ("ukv_v", 1)):
            wm = WMat(S, nc, nm_, 1, 2, 1024)
            for k in range(2):
                wm.srcs.setdefault(0, []).append((("hd", k, 0, 128), ukv4[:, k, :, sel, :]))
            W[nm_] = wm
    def conv_g4():
        conv_ffn(2)
    def conv_g5():
        W["dq"] = mk_plain("dq", w_dq, 8, 512, 1)
        uq4 = kview(w_uq).rearrange("p k (h e) -> p k h e", h=NH)
        wm = WMat(S, nc, "uq_n", 1, 4, 1024)
        for k in range(4):
            wm.srcs.setdefault(0, []).append((("hd", k, 0, 128), uq4[:, k, :, 0:128]))
        W["uq_n"] = wm
        wm = WMat(S, nc, "uq_r", 1, 4, 1024)
        for k in range(4):
            wm.srcs.setdefault(0, []).append((("hd", k, 0, 64), uq4[:, k, :, 128:192]))
            wm.srcs.setdefault(0, []).append((("hd", k, 64, 96), uq4[:, k, :, 160:192]))
            wm.srcs.setdefault(0, []).append((("hd", k, 96, 128), uq4[:, k, :, 128:160]))
        W["uq_r"] = wm
        W["wo"] = mk_plain("wo", w_o, 8, 512, 2)
    def conv_g6():
        conv_ffn(3)


    for fn_ in (lambda: conv_ffn(0), conv_g1, conv_g2, conv_g3, conv_g4, conv_g5, conv_g6):
        fn_()

    def conv_upto(n):
        pass


    GEN = (0, 1, 2, 3, 4, 6)
    SS = PS[5]
    SS2 = PS[7]
    rs_pre = [rstd[0], rstd[1]]
    rs_post = rstd[2]
    pre_state = {"i": 0}

    def finish_rstd(ps, rs, eps_col):
        S.op("act", lambda e: e.activation(out=rs.t[:], in_=ps.t[:], func=AF.Ln, bias=epsb.t[:, eps_col:eps_col + 1], scale=1.0),
             reads=[ps.b, epsb.b], writes=[rs.b])
        S.op("act", lambda e: e.activation(out=rs.t[:], in_=rs.t[:], func=AF.Exp, scale=-0.5), reads=[rs.b], writes=[rs.b])

    def rms_rstd(chunks, nfeat, rs=None, eps_col=0):
        ps = rot(GEN)
        if rs is None:
            rs = ps
        n = len(chunks)
        for i, (ap, b) in enumerate(chunks):
            sq = next_sq()
            S.op("act", lambda e, sq=sq, ap=ap: e.activation(out=sq.t[:], in_=ap, func=AF.Square),
                 reads=[b], writes=[sq.b])
            S.op("pe", lambda e, sq=sq, ps=ps, i=i: e.matmul(ps.t[:], lhsT=ones[nfeat].t[:], rhs=sq.t[:],
                                                             start=(i == 0), stop=(i == n - 1)),
                 reads=[sq.b, ones[nfeat].b], writes=[ps.b])
        finish_rstd(ps, rs, eps_col)
        return rs

    def prenorm_chunk(c, grow_idx, first, last, src=None):
        if src is None:
            sap, sbuf_ = hT.t[:, c, :], hb[c]
        else:
            sap, sbuf_ = src.t[:], src.b
        S.op("act", lambda e: e.activation(out=xn.t[:, c, :], in_=sap, func=AF.Identity, scale=gcol(grow_idx, c)),
             reads=[sbuf_, gT.b], writes=[xnb[c]])
        sl_ = SQ_SLOTS[c]
        S.op("act", lambda e: e.activation(out=AR.t[:, sl_, :], in_=sap, func=AF.Square), reads=[sbuf_], writes=[arb[sl_]])

        def mm():
            S.op("pe", lambda e: e.matmul(SS2.t[:], lhsT=ones[1024].t[:], rhs=AR.t[:, sl_, :], start=first, stop=last),
                 reads=[arb[sl_], ones[1024].b], writes=[SS2.b])
        deferred.append(mm)

    SQ_SLOTS = (22, 23, 26, 27, 28, 29, 30, 31)
    deferred = []

    def flush_deferred():
        fl = list(deferred)
        del deferred[:]
        for f_ in fl:
            f_()

    def next_rs_pre():
        rs = rs_pre[pre_state["i"] % 2]
        pre_state["i"] += 1
        return rs

    def prenorm_raw(grow_idx):
        rs = next_rs_pre()
        for c in range(8):
            prenorm_chunk(c, grow_idx, c == 0, c == 7)
        deferred.append(lambda: finish_rstd(SS2, rs, 0))
        return rs

    def act_warm_ln():
        S.op("act", lambda e: e.activation(out=warm.t[:], in_=epsb.t[:, 0:1], func=AF.Ln), reads=[epsb.b], writes=[warm.b])

    pend = {"f": None}

    def flush_pend():
        if pend["f"] is not None:
            pend["f"]()
            pend["f"] = None

    def post_cb(grow_post, half):
        on = ones[256] if half else ones[1024]

        def cb(m, ps):
            flush_pend()
            sq = next_sq()
            S.op("act", lambda e: e.activation(out=sq.t[:], in_=ps.t[:], func=AF.Square), reads=[ps.b], writes=[sq.b])
            S.op("dve", lambda e: e.tensor_scalar(out=fA.t[:, m, :], in0=ps.t[:], scalar1=gcol(grow_post, m), scalar2=None,
                                                  op0=ALU.mult), reads=[ps.b, gT.b], writes=[fb[m]])

            def mm():
                S.op("pe", lambda e: e.matmul(SS.t[:], lhsT=on.t[:], rhs=sq.t[:], start=(m == 0), stop=(m == 7)),
                     reads=[sq.b, on.b], writes=[SS.b])
            pend["f"] = mm
        return cb

    def postnorm_residual(half, next_pre):
        flush_pend()
        finish_rstd(SS, SS, 2 if half else 0)
        rs2 = next_rs_pre() if next_pre is not None else None
        for c in range(8):
            S.op("dve", lambda e, c=c: e.tensor_tensor(out=fA.t[:, c, :], in0=fA.t[:, c, :], in1=SS.t[:], op=ALU.mult),
                 reads=[fb[c], SS.b], writes=[fb[c]])
            S.op(PEW(), lambda e, c=c: e.tensor_tensor(out=hT.t[:, c, :], in0=hT.t[:, c, :], in1=fA.t[:, c, :], op=ALU.add),
                 reads=[fb[c], hb[c]], writes=[hb[c]])
            if next_pre is not None:
                prenorm_chunk(c, next_pre, c == 0, c == 7)
        if next_pre is not None:
            deferred.append(lambda: finish_rstd(SS2, rs2, 0))
        return rs2

    def gemm_fm(wm, nk, rhs, mchunks_per_slab, nm_total, cb, ksplit=1, banks=GEN, kouter=0):
        m = 0
        si = 0
        kper = nk // ksplit
        while m < nm_total:
            views = [load_slab(wm, si + j) for j in range(ksplit)]
            si += ksplit
            mm = 0
            if m == 0 and kouter > 1:
                pss = [rot(banks) for _ in range(kouter)]
                for k in range(nk):
                    v, vb = views[k // kper]
                    kk = k % kper
                    for q, ps in enumerate(pss):
                        S.op("pe", lambda e, ps=ps, v=v, kk=kk, q=q, k=k: e.matmul(
                            ps.t[:], lhsT=v[:, kk, q * 128:(q + 1) * 128], rhs=rhs[k][0], start=(k == 0), stop=(k == nk - 1)),
                            reads=[vb, rhs[k][1]], writes=[ps.b])
                flush_deferred()
                for q, ps in enumerate(pss):
                    cb(q, ps)
                m = kouter
                mm = kouter
            while mm < mchunks_per_slab and m < nm_total:
                ps = rot(banks)
                for k in range(nk):
                    v, vb = views[k // kper]
                    kk = k % kper
                    S.op("pe", lambda e, ps=ps, v=v, kk=kk, mm=mm, k=k: e.matmul(
                        ps.t[:], lhsT=v[:, kk, mm * 128:(mm + 1) * 128], rhs=rhs[k][0], start=(k == 0), stop=(k == nk - 1)),
                        reads=[vb, rhs[k][1]], writes=[ps.b])
                cb(m, ps)
                m += 1
                mm += 1

    load_x_ref = [None]

    def ffn(li, rs, next_pre, prefetch=None, mid_hook=None):
        gate, up, down = W["gate%d" % li], W["up%d" % li], W["down%d" % li]

        def evac_a(j, pg):
            flush_deferred()
            tm, tb = fA.t[:, j % 8, :], fb[j % 8]
            S.op("dve", lambda e: e.tensor_tensor(out=tm, in0=pg.t[:], in1=rs.t[:], op=ALU.mult),
                 reads=[pg.b, rs.b], writes=[tb])

        def evac_b(j, pu):
            tm, tb = fA.t[:, j % 8, :], fb[j % 8]
            t2, tb2 = fA.t[:, (j + 4) % 8, :], fb[(j + 4) % 8]
            S.op("dve", lambda e: e.tensor_tensor(out=t2, in0=pu.t[:], in1=rs.t[:], op=ALU.mult),
                 reads=[pu.b, rs.b], writes=[tb2])
            S.op("act", lambda e: e.activation(out=tm, in_=tm, func=AF.Silu), reads=[tb], writes=[tb])
            S.op(PEW(), lambda e: e.tensor_tensor(out=AR.t[:, j, :], in0=tm, in1=t2, op=ALU.mult),
                 reads=[tb, tb2], writes=[arb[j]])

        def evac(j, pg, pu):
            evac_a(j, pg)
            evac_b(j, pu)

        def mmj(ps, v, vb, k, mm):
            S.op("pe", lambda e: e.matmul(ps.t[:], lhsT=v[:, k, mm * 128:(mm + 1) * 128], rhs=xn.t[:, k, :],
                                          start=(k == 0), stop=(k == 7)), reads=[vb, xnb[k]], writes=[ps.b])
        for si in range(6):
            if si == 2 and mid_hook is not None:
                mid_hook()
            gv, gb = load_slab(gate, si)
            uv, ub = load_slab(up, si)
            mm0 = 0
            if si == 0 and DBG.get('kouter', 1):
                pgs = [rot(GEN) for _ in range(3)]
                pus = [rot(GEN) for _ in range(3)]
                for k in range(8):
                    for q in range(3):
                        mmj(pgs[q], gv, gb, k, q)
                        mmj(pus[q], uv, ub, k, q)
                for q in range(3):
                    evac_a(q, pgs[q])
                for q in range(3):
                    evac_b(q, pus[q])
                mm0 = 3
            for mm in range(mm0, 4):
                j = si * 4 + mm
                if j >= 22:
                    break
                pg = rot(GEN)
                pu = rot(GEN)
                for k in range(8):
                    mmj(pg, gv, gb, k, mm)
                for k in range(8):
                    mmj(pu, uv, ub, k, mm)
                evac(j, pg, pu)
        act_warm_ln()
        gemm_fm(down, 22, [(AR.t[:, j, :], arb[j]) for j in range(22)], 2, 8, post_cb(R_POST + li, True), ksplit=2)
        if prefetch is not None:
            load_x_ref[0](prefetch)
        return postnorm_residual(True, next_pre)

    def gmlp(rs, next_pre):
        xr = [(xn.t[:, k, :], xnb[k]) for k in range(8)]
        for fbk in range(4):
            v, vb = load_slab(W["win_v"], fbk)
            pss_ = [rot(GEN) for _ in range(4)]
            if fbk == 0:
                for k in range(8):
                    for n in range(4):
                        S.op("pe", lambda e, k=k, n=n, v=v, pp_=pss_[n]: e.matmul(pp_.t[:], lhsT=xn.t[:, k, n * 128:(n + 1) * 128],
                                                                      rhs=v[:, k, :], start=(k == 0), stop=(k == 7)),
                             reads=[vb, xnb[k]], writes=[pss_[n].b])
                flush_deferred()
                pst = rot(GEN)
                for n in range(4):
                    S.op("pe", lambda e, n=n: e.transpose(pst.t[:, n * 128:(n + 1) * 128], rs.t[:, n * 128:(n + 1) * 128], ident.t[:]),
                         reads=[rs.b, ident.b], writes=[pst.b])
                S.op("dve", lambda e: e.tensor_copy(out=rcol.t[:], in_=pst.t[:].rearrange("p (n t) -> p n t", n=4)[:, :, 0]),
                     reads=[pst.b], writes=[rcol.b])

            for n in range(4):
                ps = pss_[n]
                sl_ = 16 + 4 * n + fbk
                for k in range(8 if fbk > 0 else 0):
                    S.op("pe", lambda e, ps=ps, k=k, n=n, v=v: e.matmul(ps.t[:], lhsT=xn.t[:, k, n * 128:(n + 1) * 128],
                                                                   rhs=v[:, k, :], start=(k == 0), stop=(k == 7)),
                         reads=[vb, xnb[k]], writes=[ps.b])
                S.op("act", lambda e, ps=ps, sl_=sl_, n=n: e.activation(out=AR.t[:, sl_, :], in_=ps.t[:],
                                                                        func=AF.Gelu_apprx_tanh, scale=rcol.t[:, n:n + 1]),
                     reads=[ps.b, rcol.b], writes=[arb[sl_]])
                S.op("dve", lambda e, sl_=sl_, n=n, fbk=fbk: e.bn_stats(out=stat6.t[:, n * 4 + fbk, :], in_=AR.t[:, sl_, :]),
                     reads=[arb[sl_]], writes=[stat6.b])
        for n in range(4):
            S.op("dve", lambda e, n=n: e.bn_aggr(out=mv.t[:, n, :], in_=stat6.t[:, n * 4:(n + 1) * 4, :]),
                 reads=[stat6.b], writes=[mv.b])
        S.op("act", lambda e: e.activation(out=rsd.t[:, 0, :], in_=mv.t[:, :, 1], func=AF.Sqrt, bias=epsb.t[:, 1:2], scale=1.0),
             reads=[mv.b, epsb.b], writes=[rsd.b])
        S.op("dve", lambda e: e.reciprocal(out=rsd.t[:, 0, :], in_=rsd.t[:, 0, :]), reads=[rsd.b], writes=[rsd.b])
        S.op("dve", lambda e: e.scalar_tensor_tensor(out=rsd.t[:, 1, :], in0=mv.t[:, :, 0], scalar=-1.0, in1=rsd.t[:, 0, :],
                                                     op0=ALU.mult, op1=ALU.mult),
             reads=[mv.b, rsd.b], writes=[rsd.b])
        for n in range(4):
            vv_ = AR.t[:, 16 + 4 * n:20 + 4 * n, :].rearrange("p a b -> p (a b)")
            S.op("dve", lambda e, n=n, vv_=vv_: e.tensor_scalar(out=vv_, in0=vv_, scalar1=rsd.t[:, 0, n:n + 1],
                                                                scalar2=rsd.t[:, 1, n:n + 1], op0=ALU.mult, op1=ALU.add),
                 reads=[rsd.b] + [arb[16 + 4 * n + i] for i in range(4)], writes=[arb[16 + 4 * n + i] for i in range(4)])
        def cb_u(m, ps):
            tm = tmpA if m % 2 == 0 else tmpB
            S.op("dve", lambda e: e.tensor_tensor(out=tm.t[:], in0=ps.t[:], in1=rs.t[:], op=ALU.mult),
                 reads=[ps.b, rs.b], writes=[tm.b])
            S.op("act", lambda e: e.activation(out=AR.t[:, m, :], in_=tm.t[:], func=AF.Gelu_apprx_tanh),
                 reads=[tm.b], writes=[arb[m]])
        gemm_fm(W["win_u"], 8, xr, 4, 16, cb_u)
        act_warm_ln()
        wo_ = W["wout"]
        wv = [load_slab(wo_, 0), load_slab(wo_, 1)]
        acc = [PS[q] for q in range(4)]
        sp_banks = (4, 6)

        def spatial(g):
            ps = rot(sp_banks)
            for n in range(4):
                s_ = 16 + 4 * n + g // 4
                S.op("pe", lambda e, n=n, s_=s_: e.matmul(
                    ps.t[:, n * 128:(n + 1) * 128], lhsT=AR.t[:, s_, (g % 4) * 128:(g % 4 + 1) * 128],
                    rhs=WmT.t[:, g, :], start=True, stop=True),
                    reads=[arb[s_], WmT.b], writes=[ps.b])
            tm = tmpA if g % 2 == 0 else tmpB
            S.op("dve", lambda e: e.scalar_tensor_tensor(
                out=tm.t[:].rearrange("p (n t) -> p n t", n=4), in0=ps.t[:].rearrange("p (n t) -> p n t", n=4),
                scalar=gcol(R_LNG + g // 8, g % 8),
                in1=Ctab.t[:, g:g + 1, :].to_broadcast([128, 4, 128]), op0=ALU.mult, op1=ALU.add),
                reads=[ps.b, Ctab.b, gT.b], writes=[tm.b])
            S.op(PEW(), lambda e: e.tensor_tensor(out=AR.t[:, g, :], in0=AR.t[:, g, :], in1=tm.t[:], op=ALU.mult),
                 reads=[arb[g], tm.b], writes=[arb[g]])

        def wout_k(g):
            for q in range(4):
                v, vb = wv[q // 2]
                S.op("pe", lambda e, q=q, v=v: e.matmul(acc[q].t[:], lhsT=v[:, g, (q % 2) * 128:(q % 2 + 1) * 128],
                                                       rhs=AR.t[:, g, :], start=(g == 0), stop=(g == 15)),
                     reads=[vb, arb[g]], writes=[acc[q].b])
        for i in range(16 + 2):
            if i < 16:
                spatial(i)
            if i - 2 >= 0:
                wout_k(i - 2)
        pcb = post_cb(R_MPOST + 0, False)
        for q in range(4):
            pcb(q, acc[q])
        for si in (2, 3):
            v, vb = load_slab(wo_, si)
            for mm in range(2):
                ps = rot(GEN)
                for k in range(16):
                    S.op("pe", lambda e, ps=ps, v=v, k=k, mm=mm: e.matmul(ps.t[:], lhsT=v[:, k, mm * 128:(mm + 1) * 128],
                                                                       rhs=AR.t[:, k, :], start=(k == 0), stop=(k == 15)),
                         reads=[vb, arb[k]], writes=[ps.b])
                pcb(si * 2 + mm, ps)
        return postnorm_residual(False, next_pre)

    def rope_tables(tok0):
        S.dma("pool", ang.t[:], pos[0:1, tok0:tok0 + T].partition_broadcast(64), io_ch[2], writes=[ang.b])
        S.op("dve", lambda e: e.tensor_scalar(out=ang.t[:], in0=ang.t[:], scalar1=ropec.t[:, 0:1], scalar2=None, op0=ALU.mult),
             reads=[ang.b, ropec.b], writes=[ang.b])

        def reduce_sin(dst, shift):
            S.op("dve", lambda e: e.tensor_scalar(out=angq.t[:], in0=ang.t[:], scalar1=float(shift), scalar2=float(1 / (2 * np.pi)),
                                                  op0=ALU.add, op1=ALU.mult), reads=[ang.b], writes=[angq.b])
            S.op("dve", lambda e: e.tensor_scalar(out=angq.t[:], in0=angq.t[:], scalar1=MAGIC, scalar2=None, op0=ALU.add),
                 reads=[angq.b], writes=[angq.b])
            S.op("dve", lambda e: e.tensor_scalar(out=angq.t[:], in0=angq.t[:], scalar1=-MAGIC, scalar2=None, op0=ALU.add),
                 reads=[angq.b], writes=[angq.b])
            S.op("dve", lambda e: e.scalar_tensor_tensor(out=angq.t[:], in0=angq.t[:], scalar=float(-2 * np.pi), in1=ang.t[:],
                                                         op0=ALU.mult, op1=ALU.add), reads=[angq.b, ang.b], writes=[angq.b])
            S.op("dve", lambda e: e.tensor_scalar(out=angq.t[:], in0=angq.t[:], scalar1=float(shift), scalar2=PI_LO,
                                                  op0=ALU.add, op1=ALU.min), reads=[angq.b], writes=[angq.b])
            S.op("dve", lambda e: e.tensor_scalar(out=angq.t[:], in0=angq.t[:], scalar1=-PI_LO, scalar2=None, op0=ALU.max),
                 reads=[angq.b], writes=[angq.b])
            S.op("act", lambda e: e.activation(out=dst.t[:], in_=angq.t[:], func=AF.Sin), reads=[angq.b], writes=[dst.b])
        reduce_sin(sin2, 0.0)
        reduce_sin(cos2, float(np.pi / 2))
        S.op("dve", lambda e: e.tensor_scalar(out=sin2.t[:], in0=sin2.t[:], scalar1=ropec.t[:, 1:2], scalar2=None, op0=ALU.mult),
             reads=[sin2.b, ropec.b], writes=[sin2.b])

    def rope_apply(px, psw, ct, st, dst_ap, dst_bufs, rs=None):
        S.op("dve", lambda e: e.tensor_tensor(out=tmpA.t[0:64, :], in0=px.t[0:64, :], in1=ct.t[:], op=ALU.mult),
             reads=[px.b, ct.b], writes=[tmpA.b])
        S.op("dve", lambda e: e.tensor_tensor(out=tmpB.t[0:64, :], in0=psw.t[0:64, :], in1=st.t[:], op=ALU.mult),
             reads=[psw.b, st.b], writes=[tmpB.b])
        if rs is None:
            S.op(PEW(), lambda e: e.tensor_tensor(out=dst_ap, in0=tmpA.t[0:64, :], in1=tmpB.t[0:64, :], op=ALU.add),
                 reads=[tmpA.b, tmpB.b], writes=dst_bufs)
        else:
            S.op(PEW(), lambda e: e.tensor_tensor(out=tmpA.t[0:64, :], in0=tmpA.t[0:64, :], in1=tmpB.t[0:64, :], op=ALU.add),
                 reads=[tmpA.b, tmpB.b], writes=[tmpA.b])
            S.op("dve", lambda e: e.tensor_tensor(out=dst_ap, in0=tmpA.t[0:64, :], in1=rs.t[0:64, :], op=ALU.mult),
                 reads=[tmpA.b, rs.b], writes=dst_bufs)

    def shared_kv(ti, rs, next_pre):
        t0 = ti * T
        dv, db = load_slab(W["dkv"], 0)
        pcs = [rot(GEN), rot(GEN)]
        px, psw = rot(GEN), rot(GEN)
        for k in range(8):
            for m in range(2):
                S.op("pe", lambda e, k=k, m=m: e.matmul(pcs[m].t[:], lhsT=dv[:, k, m * 128:(m + 1) * 128], rhs=xn.t[:, k, :],
                                                        start=(k == 0), stop=(k == 7)), reads=[db, xnb[k]], writes=[pcs[m].b])
            for pp, c0 in ((px, 256), (psw, 320)):
                S.op("pe", lambda e, pp=pp, k=k, c0=c0: e.matmul(pp.t[0:64, :], lhsT=dv[:, k, c0:c0 + 64], rhs=xn.t[:, k, :],
                                                                start=(k == 0), stop=(k == 7)), reads=[db, xnb[k]], writes=[pp.b])
        flush_deferred()
        for m in range(2):
            S.op("dve", lambda e, m=m: e.tensor_tensor(out=fA.t[:, m, :], in0=pcs[m].t[:], in1=rs.t[:], op=ALU.mult),
                 reads=[pcs[m].b, rs.b], writes=[fb[m]])
        rope_apply(px, psw, cos2, sin2, KR.t[0:64, t0:t0 + T], [KRb[ti]], rs=rs)
        rsc = rms_rstd([(fA.t[:, m, :], fb[m]) for m in range(2)], 256)
        for m in range(2):
            S.op("dve", lambda e, m=m: e.scalar_tensor_tensor(
                out=AR.t[:, 24 + m, :], in0=fA.t[:, m, :], scalar=gcol(R_KVA, m), in1=rsc.t[:], op0=ALU.mult, op1=ALU.mult),
                 reads=[fb[m], rsc.b, gT.b], writes=[arb[24 + m]])
        rs_next = prenorm_raw(next_pre) if next_pre is not None else None
        cn = [(AR.t[:, 24 + m, :], arb[24 + m]) for m in range(2)]

        def cb_k(h, ps):
            S.op("dve", lambda e, h=h, ps=ps: e.tensor_copy(out=KT.t[:, h, t0:t0 + T], in_=ps.t[:]),
                 reads=[ps.b], writes=[KTb[h][ti]])
        gemm_fm(W["ukv_k"], 2, cn, 8, 8, cb_k, kouter=6)
        vv, vb = load_slab(W["ukv_v"], 0)
        for n in range(4):
            for half in range(2):
                ps = rot(GEN)
                for k in range(2):
                    S.op("pe", lambda e, ps=ps, k=k, n=n, half=half: e.matmul(
                        ps.t[:], lhsT=AR.t[:, 24 + k, n * 128:(n + 1) * 128], rhs=vv[:, k, half * 512:(half + 1) * 512],
                        start=(k == 0), stop=(k == 1)), reads=[vb, arb[24 + k]], writes=[ps.b])
                if True:
                    S.op("dve", lambda e, ps=ps, n=n, half=half: e.tensor_copy(
                        out=VS.t[:, ti * 4 + n, half * 512:(half + 1) * 512], in_=ps.t[:]),
                        reads=[ps.b], writes=[VSb[ti * 4 + n]])
                else:
                    S.op("act", lambda e, ps=ps, n=n, half=half: e.activation(
                        out=VS.t[:, ti * 4 + n, half * 512:(half + 1) * 512], in_=ps.t[:], func=AF.Copy),
                        reads=[ps.b], writes=[VSb[ti * 4 + n]])
        flush_deferred()
        return rs_next

    def mla(ti, rs, next_pre):
        xr = [(xn.t[:, k, :], xnb[k]) for k in range(8)]

        def cb_dq(m, ps):
            S.op("dve", lambda e, m=m, ps=ps: e.tensor_tensor(out=fA.t[:, m, :], in0=ps.t[:], in1=rs.t[:], op=ALU.mult),
                 reads=[ps.b, rs.b], writes=[fb[m]])
        gemm_fm(W["dq"], 8, xr, 4, 4, cb_dq, kouter=4)
        rsq = rms_rstd([(fA.t[:, m, :], fb[m]) for m in range(4)], 512)
        for m in range(4):
            S.op("dve", lambda e, m=m: e.scalar_tensor_tensor(
                out=AR.t[:, 24 + m, :], in0=fA.t[:, m, :], scalar=gcol(R_QN, m), in1=rsq.t[:], op0=ALU.mult, op1=ALU.mult),
                 reads=[fb[m], rsq.b, gT.b], writes=[arb[24 + m]])
        cq = [(AR.t[:, 24 + m, :], arb[24 + m]) for m in range(4)]

        def cb_qn(h, ps):
            S.op("act", lambda e, h=h, ps=ps: e.activation(out=AR.t[:, h, :], in_=ps.t[:], func=AF.Copy, scale=SCALE),
                 reads=[ps.b], writes=[arb[h]])
        gemm_fm(W["uq_n"], 4, cq, 8, 8, cb_qn, kouter=6)
        rv, rb = load_slab(W["uq_r"], 0)
        for tb_ in (cos2, sin2):
            S.op("dve", lambda e, tb_=tb_: e.tensor_scalar(out=tb_.t[:], in0=tb_.t[:], scalar1=SCALE, scalar2=None, op0=ALU.mult),
                 reads=[tb_.b], writes=[tb_.b])
        S.op(PEW(), lambda e: e.memset(AR.t[64:65, 8:16, :], 0.0), writes=[arb[8 + h] for h in range(NH)])
        for h in range(NH):
            px, psw = rot(GEN), rot(GEN)
            for pp, c0 in ((px, 0), (psw, 64)):
                for k in range(4):
                    S.op("pe", lambda e, pp=pp, k=k, c0=c0, h=h: e.matmul(
                        pp.t[0:64, :], lhsT=rv[:, k, h * 128 + c0:h * 128 + c0 + 64], rhs=AR.t[:, 24 + k, :],
                        start=(k == 0), stop=(k == 3)), reads=[rb, arb[24 + k]], writes=[pp.b])
            rope_apply(px, psw, cos2, sin2, AR.t[0:64, 8 + h, :], [arb[8 + h]])
        nkb = 4 * (ti + 1)
        SK = 2
        items = [(h, kb) for h in range(NH) for kb in range(nkb)]
        st = {}

        def emit_scores(idx):
            h, kb = items[idx]
            jd = kb - 4 * ti
            q0 = 0 if jd < 0 else jd * 128
            kt_i = kb // 4
            pss = rot((0, 1, 2, 3))
            S.op("pe", lambda e: e.matmul(pss.t[:, q0:T], lhsT=KT.t[:, h, kb * 128:(kb + 1) * 128], rhs=AR.t[:, h, q0:T],
                                          start=True, stop=False),
                 reads=[KTb[h][kt_i], arb[h]], writes=[pss.b])
            S.op("pe", lambda e: e.matmul(pss.t[:, q0:T], lhsT=KR.t[0:65, kb * 128:(kb + 1) * 128], rhs=AR.t[0:65, 8 + h, q0:T],
                                          start=False, stop=True),
                 reads=[KRb[kt_i], KR1b, arb[8 + h]], writes=[pss.b])
            pt = ptile[idx % 3]
            S.op("act", lambda e: e.activation(out=pt.t[:, q0:T], in_=pss.t[:, q0:T], func=AF.Exp),
                 reads=[pss.b], writes=[pt.b])
            if jd >= 0:
                S.op(PEW(), lambda e: e.tensor_tensor(out=pt.t[:, q0:q0 + 128], in0=pt.t[:, q0:q0 + 128],
                                                       in1=tri.t[:], op=ALU.mult),
                     reads=[pt.b, tri.b], writes=[pt.b])
            st[idx] = (pt, q0)

        def emit_pv(idx):
            h, kb = items[idx]
            pt, q0 = st.pop(idx)
            po = PS[4 + 2 * (h % 2)]
            pd = PS[5 + 2 * (h % 2)]
            S.op("pe", lambda e: e.matmul(po.t[:, q0:T], lhsT=VS.t[:, kb, h * 128:(h + 1) * 128], rhs=pt.t[:, q0:T],
                                          start=(kb == 0), stop=(kb == nkb - 1)),
                 reads=[VSb[kb], pt.b], writes=[po.b])
            S.op("pe", lambda e: e.matmul(pd.t[:, q0:T], lhsT=ones[1].t[:], rhs=pt.t[:, q0:T],
                                          start=(kb == 0), stop=(kb == nkb - 1)),
                 reads=[ones[1].b, pt.b], writes=[pd.b])
            if kb == nkb - 1:
                S.op("dve", lambda e: e.reciprocal(out=tmpA.t[:], in_=pd.t[:]), reads=[pd.b], writes=[tmpA.b])
                S.op("dve", lambda e: e.tensor_tensor(out=AR.t[:, 16 + h, :], in0=po.t[:], in1=tmpA.t[:], op=ALU.mult),
                     reads=[po.b, tmpA.b], writes=[arb[16 + h]])

        for i in range(len(items) + SK):
            if i < len(items):
                emit_scores(i)
            if i - SK >= 0:
                emit_pv(i - SK)

        gemm_fm(W["wo"], 8, [(AR.t[:, 16 + h, :], arb[16 + h]) for h in range(NH)], 4, 8, post_cb(R_MPOST + 1, False),
                banks=(0, 1, 2, 3), kouter=4)
        return postnorm_residual(False, next_pre)

    fx = fA.t[:].rearrange("p c t -> p (c t)").rearrange("p (n d) -> p n d", n=4)
    xin = ARf[:, 0:4096].rearrange("p (n d) -> p n d", n=4)
    xin_b = arb[0:16]
    tiles = [(s_, ti) for s_ in range(nseq) for ti in range(ntile)]

    def load_x(idx):
        s_, ti = tiles[idx]
        tok0 = (s_ * ntile + ti) * T
        S.dma("sp", xin, x[tok0:tok0 + T, :].rearrange("(n p) d -> p n d", p=128), io_ch[0], writes=xin_b)

    load_x_ref[0] = load_x
    load_x(0)
    for idx, (s, ti) in enumerate(tiles):
        if True:
            tok0 = (s * ntile + ti) * T
            first_tile["v"] = (idx == 0)
            for c in range(8):
                ps = rot()
                for n in range(4):
                    S.op("pe", lambda e, ps=ps, n=n, c=c: e.transpose(ps.t[:, n * 128:(n + 1) * 128],
                                                                   xin[:, n, c * 128:(c + 1) * 128], ident.t[:]),
                         reads=xin_b + [ident.b], writes=[ps.b])
                S.op("dve", lambda e, ps=ps, c=c: e.tensor_copy(out=hT.t[:, c, :], in_=ps.t[:]), reads=[ps.b], writes=[hb[c]])
                prenorm_chunk(c, R_PRE + 0, c == 0, c == 7, src=ps)
            rs = next_rs_pre()
            deferred.append(lambda rs=rs: finish_rstd(SS2, rs, 0))
            if stage >= 1:
                rs = ffn(0, rs, R_MPRE + 0, mid_hook=(lambda tok0=tok0: rope_tables(tok0)) if stage >= 4 else None)
            if stage >= 2:
                rs = gmlp(rs, R_PRE + 1)
            if stage >= 3:
                rs = ffn(1, rs, R_KV)
            if stage >= 4:
                rs = shared_kv(ti, rs, R_PRE + 2)
            if stage >= 5:
                rs = ffn(2, rs, R_MPRE + 1)
            if stage >= 6:
                rs = mla(ti, rs, R_PRE + 3)
            if stage >= 7:
                ffn(3, rs, None, prefetch=(idx + 1) if idx + 1 < len(tiles) else None)
            elif idx + 1 < len(tiles):
                load_x(idx + 1)
            flush_deferred()
            for c in range(8):
                ps = rot()
                for n in range(4):
                    S.op("pe", lambda e, ps=ps, n=n, c=c: e.transpose(ps.t[:, n * 128:(n + 1) * 128],
                                                                   hT.t[:, c, n * 128:(n + 1) * 128], ident.t[:]),
                         reads=[hb[c], ident.b], writes=[ps.b])
                if c % 2 == 0:
                    S.op("act", lambda e, ps=ps, c=c: e.activation(out=fx[:, :, c * 128:(c + 1) * 128],
                                                                   in_=ps.t[:].rearrange("p (n d) -> p n d", n=4), func=AF.Copy),
                         reads=[ps.b], writes=fb)
                else:
                    S.op("dve", lambda e, ps=ps, c=c: e.tensor_copy(out=fx[:, :, c * 128:(c + 1) * 128],
                                                                    in_=ps.t[:].rearrange("p (n d) -> p n d", n=4)),
                         reads=[ps.b], writes=fb)
            S.dma("pool", out[tok0:tok0 + T, :].rearrange("(n p) d -> p n d", p=128), fx, io_ch[1], reads=fb)
    build_program.sbuf_left = nc.sbuf_bytes_remaining
    S.finish(final_eng="pool")
    return nc


def rope_consts():
    inv = (10000.0 ** (-np.arange(0, 64, 2, dtype=np.float32) / 64)).astype(np.float32)
    c = np.zeros((64, 2), np.float32)
    c[:, 0] = np.concatenate([inv, inv])
    c[:32, 1] = -1.0
    c[32:, 1] = 1.0
    return c


def make_in_maps(inputs, ncores, nseq, ntile):
    f = lambda a: np.ascontiguousarray(np.asarray(a))
    S_ = ntile * T
    shared = {
        "ffn_pre_g": f(inputs["ffn_pre_g"]).reshape(4, D),
        "ffn_post_g": f(inputs["ffn_post_g"]).reshape(4, D),
        "ffn_w_gate": f(inputs["ffn_w_gate"]).reshape(4, D, DFF),
        "ffn_w_up": f(inputs["ffn_w_up"]).reshape(4, D, DFF),
        "ffn_w_down": f(inputs["ffn_w_down"]).reshape(4, DFF, D),
        "mix_pre_g": f(inputs["mix_pre_g"]),
        "mix_post_g": f(inputs["mix_post_g"]),
        "gmlp_w_in": f(inputs["gmlp_w_in"]).reshape(D, 4096),
        "gmlp_ln_g": f(inputs["gmlp_ln_g"]).reshape(2, D),
        "gmlp_ln_b": f(inputs["gmlp_ln_b"]).reshape(2, D),
        "gmlp_w_s": f(inputs["gmlp_w_s"]).reshape(16, 128, 128),
        "gmlp_b_s": f(inputs["gmlp_b_s"]).reshape(1, 2048),
        "gmlp_w_out": f(inputs["gmlp_w_out"]).reshape(2048, D),
        "kv_norm_g": f(inputs["kv_norm_g"]).reshape(1, D),
        "w_dkv": f(inputs["w_dkv"]),
        "kv_a_norm_g": f(inputs["kv_a_norm_g"]).reshape(1, 256),
        "w_ukv": f(inputs["w_ukv"]),
        "mla_w_dq": f(inputs["mla_w_dq"]).reshape(D, 512),
        "mla_q_norm_g": f(inputs["mla_q_norm_g"]).reshape(1, 512),
        "mla_w_uq": f(inputs["mla_w_uq"]).reshape(512, 1536),
        "mla_w_o": f(inputs["mla_w_o"]).reshape(D, D),
        "rope_c": rope_consts(),
    }
    xs = np.asarray(inputs["x"])
    ps = np.asarray(inputs["positions"])
    maps = []
    for c in range(ncores):
        m = dict(shared)
        m["x"] = np.ascontiguousarray(xs[c * nseq:(c + 1) * nseq, :S_, :]).reshape(nseq * S_, D)
        m["pos"] = np.ascontiguousarray(ps[c * nseq:(c + 1) * nseq, :S_]).reshape(1, nseq * S_).astype(np.int32)
        maps.append(m)
    return maps


_NC_CACHE = {}


def run(inputs, ncores=8, nseq=2, ntile=4, stage=99):
    key = (nseq, ntile, stage)
    if key not in _NC_CACHE:
        _NC_CACHE[key] = build_program(nseq, ntile, stage)
    nc = _NC_CACHE[key]
    maps = make_in_maps(inputs, ncores, nseq, ntile)
    res = run_bass_kernel_spmd(nc, maps, core_ids=list(range(ncores)))
    outs = [np.asarray(r["out"]).reshape(nseq, ntile * T, D) for r in res.results]
    return np.concatenate(outs, axis=0)


def kernel(**inputs):
    return run(inputs).astype(np.float32)
```
